# Optimizing a Trainium2 kernel written in Bass

```python
import math
import jax, jax.numpy as jnp
from jax import lax
import numpy as np

D_MODEL = 1024
BATCH = 8
SEQ = 4096
DEPTH = 2

N_MIXERS = 2
N_ATTN_LAYERS = (DEPTH + 1) // 2
N_HGRN_LAYERS = DEPTH // 2

ATT_HEADS = 16
ATT_HEAD_DIM = 64
ATT_KV_HEADS = 2
Q_LORA_RANK = 256
IDX_HEADS = 8
IDX_DIM = 128
INDEX_TOPK = 256
Q_BLOCK = 128
ATT_KV_DIM = ATT_KV_HEADS * ATT_HEAD_DIM
ATT_IN_DIM = Q_LORA_RANK + 2 * ATT_KV_DIM + IDX_DIM + IDX_HEADS

HGRN_EXPAND = 128
HGRN_HEADS = D_MODEL // HGRN_EXPAND
HGRN_F_DIM = HGRN_HEADS * HGRN_EXPAND
HGRN_V_DIM = D_MODEL // HGRN_HEADS
HGRN_CHUNK = 64
HGRN_IN_DIM = 2 * HGRN_F_DIM + D_MODEL + D_MODEL

D_FF = -(-8 * D_MODEL // (3 * 256)) * 256

ROPE_THETA = 500000.0
ATT_ROT_DIM = ATT_HEAD_DIM // 4
IDX_ROT_DIM = IDX_DIM // 4
DEEPNORM_ALPHA = (2 * DEPTH) ** 0.25
DEEPNORM_BETA = (8 * DEPTH) ** -0.25
LN_EPS = 1e-5
RMS_EPS = 1e-6

kernel_name = "dsa_hgrn2_interleaved_deepnorm"


def layer_norm(x, g, b):
    xf = x.astype(jnp.float32)
    mu = jnp.mean(xf, axis=-1, keepdims=True)
    var = jnp.mean(jnp.square(xf - mu), axis=-1, keepdims=True)
    return ((xf - mu) * lax.rsqrt(var + LN_EPS) * g.astype(jnp.float32) + b.astype(jnp.float32)).astype(x.dtype)


def rms_norm(x, g):
    xf = x.astype(jnp.float32)
    ms = jnp.mean(jnp.square(xf), axis=-1, keepdims=True)
    return (xf * lax.rsqrt(ms + RMS_EPS) * g.astype(jnp.float32)).astype(x.dtype)


def rope_tables(positions, rot_dim):
    inv = ROPE_THETA ** (-jnp.arange(0, rot_dim, 2, dtype=jnp.float32) / rot_dim)
    ang = positions.astype(jnp.float32)[..., None] * inv
    return jnp.cos(ang), jnp.sin(ang)


def apply_rope(x, cos, sin):
    half = cos.shape[-1]
    cos = cos.astype(x.dtype)
    sin = sin.astype(x.dtype)
    x1 = x[..., :half]
    x2 = x[..., half:2 * half]
    return jnp.concatenate([x1 * cos - x2 * sin, x2 * cos + x1 * sin, x[..., 2 * half:]], axis=-1)


def dsa_mixer(x, cos_q, sin_q, cos_i, sin_i, w_in, g_cq, w_uq, w_iq, g_ik, b_ik, w_o):
    B, T, _ = x.shape
    dt = x.dtype
    h = x @ w_in
    c_q, k, v, k_idx, w_idx = jnp.split(
        h, [Q_LORA_RANK, Q_LORA_RANK + ATT_KV_DIM, Q_LORA_RANK + 2 * ATT_KV_DIM,
            Q_LORA_RANK + 2 * ATT_KV_DIM + IDX_DIM], axis=-1)
    c_q = rms_norm(c_q, g_cq)
    q = (c_q @ w_uq).reshape(B, T, ATT_HEADS, ATT_HEAD_DIM)
    q_idx = (c_q @ w_iq).reshape(B, T, IDX_HEADS, IDX_DIM)
    k = k.reshape(B, T, ATT_KV_HEADS, ATT_HEAD_DIM)
    v = v.reshape(B, T, ATT_KV_HEADS, ATT_HEAD_DIM)
    k_idx = layer_norm(k_idx, g_ik, b_ik)
    q = apply_rope(q, cos_q[:, :, None], sin_q[:, :, None])
    k = apply_rope(k, cos_q[:, :, None], sin_q[:, :, None])
    q_idx = apply_rope(q_idx, cos_i[:, :, None], sin_i[:, :, None])
    k_idx = apply_rope(k_idx, cos_i, sin_i)
    w_idx = w_idx * (IDX_HEADS ** -0.5 * IDX_DIM ** -0.5)

    topk = min(INDEX_TOPK, T // 4)
    nb = T // Q_BLOCK
    group = ATT_HEADS // ATT_KV_HEADS
    key_pos = jnp.arange(T)

    def blocks(a):
        return jnp.moveaxis(a.reshape(B, nb, Q_BLOCK, *a.shape[2:]), 1, 0)

    def attend_block(args):
        qb, qib, wb, start = args
        t_pos = start + jnp.arange(Q_BLOCK)
        causal = key_pos[None, :] <= t_pos[:, None]
        rel = jax.nn.relu(jnp.einsum('bqhd,bsd->bhqs', qib, k_idx))
        score = jnp.einsum('bqh,bhqs->bqs', wb, rel).astype(jnp.float32)
        score = jnp.where(causal[None], score, -jnp.inf)
        _, sel = lax.top_k(score, topk)
        valid = sel <= t_pos[None, :, None]
        k_sel = jax.vmap(lambda kk, ii: kk[ii])(k, sel)
        v_sel = jax.vmap(lambda vv, ii: vv[ii])(v, sel)
        qg = qb.reshape(B, Q_BLOCK, ATT_KV_HEADS, group, ATT_HEAD_DIM)
        logits = jnp.einsum('bqgrd,bqkgd->bqgrk', qg, k_sel).astype(jnp.float32) * (ATT_HEAD_DIM ** -0.5)
        logits = jnp.where(valid[:, :, None, None, :], logits, -jnp.inf)
        p = jax.nn.softmax(logits, axis=-1).astype(dt)
        o = jnp.einsum('bqgrk,bqkgd->bqgrd', p, v_sel)
        return o.reshape(B, Q_BLOCK, ATT_HEADS * ATT_HEAD_DIM)

    out = lax.map(attend_block, (blocks(q), blocks(q_idx), blocks(w_idx), jnp.arange(nb) * Q_BLOCK))
    out = jnp.moveaxis(out, 0, 1).reshape(B, T, ATT_HEADS * ATT_HEAD_DIM)
    return out @ w_o


def chunk_gated_linear(q, k, v, logf):
    B, H, T, K = q.shape
    V = v.shape[-1]
    C = HGRN_CHUNK
    nc = T // C
    mask = jnp.tril(jnp.ones((C, C), dtype=bool))

    def to_chunks(a):
        return jnp.moveaxis(a.reshape(B, H, nc, C, a.shape[-1]), 2, 0)

    def step(S, inp):
        qc, kc, vc, gc = inp
        b = jnp.cumsum(gc, axis=-2)
        diff = b[..., :, None, :] - b[..., None, :, :]
        decay = jnp.exp(jnp.where(mask[:, :, None], diff, -jnp.inf))
        attn = jnp.einsum('bhtsk,bhsk->bhts', decay * qc[..., :, None, :], kc)
        o = jnp.einsum('bhts,bhsv->bhtv', attn, vc) + jnp.einsum('bhtk,bhkv->bhtv', qc * jnp.exp(b), S)
        b_last = b[..., -1:, :]
        S = jnp.exp(b_last[..., 0, :])[..., None] * S + jnp.einsum('bhsk,bhsv->bhkv', kc * jnp.exp(b_last - b), vc)
        return S, o

    S0 = jnp.zeros((B, H, K, V), jnp.float32)
    _, o = lax.scan(step, S0, (to_chunks(q), to_chunks(k), to_chunks(v), to_chunks(logf)))
    return jnp.moveaxis(o, 0, 2).reshape(B, H, T, V)


def hgrn2_mixer(x, lb, w_in, g_norm, w_o):
    B, T, _ = x.shape
    dt = x.dtype
    h = x @ w_in
    q, f, i, g = jnp.split(h, [HGRN_F_DIM, 2 * HGRN_F_DIM, 2 * HGRN_F_DIM + D_MODEL], axis=-1)
    q = jax.nn.silu(q.astype(jnp.float32))
    fg = lb + (1.0 - lb) * jax.nn.sigmoid(f.astype(jnp.float32))
    k = 1.0 - fg
    logf = jnp.log(fg)

    def heads(a, d):
        return jnp.transpose(a.reshape(B, T, HGRN_HEADS, d), (0, 2, 1, 3))

    o = chunk_gated_linear(heads(q, HGRN_EXPAND), heads(k, HGRN_EXPAND),
                           heads(i.astype(jnp.float32), HGRN_V_DIM), heads(logf, HGRN_EXPAND))
    o = jnp.transpose(o, (0, 2, 1, 3))
    o = rms_norm(o, g_norm) * jax.nn.silu(g.astype(jnp.float32).reshape(B, T, HGRN_HEADS, HGRN_V_DIM))
    return o.reshape(B, T, D_MODEL).astype(dt) @ w_o


def swiglu(x, w_gate, w_up, w_down):
    return (jax.nn.silu(x @ w_gate) * (x @ w_up)) @ w_down


def setup_inputs(seed: int = 0) -> dict:
    key = jax.random.key(seed)
    ks = jax.random.split(key, 32)
    f32 = jnp.float32
    nA, nB, D = N_ATTN_LAYERS, N_HGRN_LAYERS, D_MODEL

    def nrm(k, shape, fan_in, scale=1.0):
        return jax.random.normal(k, shape, f32) * (fan_in ** -0.5) * scale

    x = jax.random.normal(ks[0], (BATCH, SEQ, D), f32)
    offset = jax.random.randint(ks[1], (BATCH, 1), 0, 1024, dtype=jnp.int32)
    positions = offset + jnp.arange(SEQ, dtype=jnp.int32)[None, :]

    att_w_in = jnp.concatenate([
        nrm(ks[2], (nA, D, Q_LORA_RANK), D),
        nrm(ks[3], (nA, D, ATT_KV_DIM), D),
        nrm(ks[4], (nA, D, ATT_KV_DIM), D, DEEPNORM_BETA),
        nrm(ks[5], (nA, D, IDX_DIM), D),
        nrm(ks[6], (nA, D, IDX_HEADS), D)], axis=-1)
    att_g_cq = 1.0 + 0.02 * jax.random.normal(ks[7], (nA, Q_LORA_RANK), f32)
    att_w_uq = nrm(ks[8], (nA, Q_LORA_RANK, ATT_HEADS * ATT_HEAD_DIM), Q_LORA_RANK)
    att_w_iq = nrm(ks[9], (nA, Q_LORA_RANK, IDX_HEADS * IDX_DIM), Q_LORA_RANK)
    att_g_ik = 1.0 + 0.02 * jax.random.normal(ks[10], (nA, IDX_DIM), f32)
    att_b_ik = 0.02 * jax.random.normal(ks[11], (nA, IDX_DIM), f32)
    att_w_o = nrm(ks[12], (nA, ATT_HEADS * ATT_HEAD_DIM, D), ATT_HEADS * ATT_HEAD_DIM, DEEPNORM_BETA)

    hgrn_lb_logits = 0.5 * jax.random.normal(ks[13], (DEPTH, HGRN_F_DIM), f32)
    hgrn_w_in = jnp.concatenate([
        nrm(ks[14], (nB, D, HGRN_F_DIM), D),
        nrm(ks[15], (nB, D, HGRN_F_DIM), D),
        nrm(ks[16], (nB, D, D), D, DEEPNORM_BETA),
        nrm(ks[17], (nB, D, D), D)], axis=-1)
    hgrn_g_norm = 1.0 + 0.02 * jax.random.normal(ks[18], (nB, HGRN_V_DIM), f32)
    hgrn_w_o = nrm(ks[19], (nB, D, D), D, DEEPNORM_BETA)

    ffn_w_gate = nrm(ks[20], (DEPTH, D, D_FF), D, DEEPNORM_BETA)
    ffn_w_up = nrm(ks[21], (DEPTH, D, D_FF), D, DEEPNORM_BETA)
    ffn_w_down = nrm(ks[22], (DEPTH, D_FF, D), D_FF, DEEPNORM_BETA)

    ln_g = 1.0 + 0.02 * jax.random.normal(ks[23], (DEPTH, 2, D), f32)
    ln_b = 0.02 * jax.random.normal(ks[24], (DEPTH, 2, D), f32)

    return {"x": x, "positions": positions,
            "att_w_in": att_w_in, "att_g_cq": att_g_cq, "att_w_uq": att_w_uq, "att_w_iq": att_w_iq,
            "att_g_ik": att_g_ik, "att_b_ik": att_b_ik, "att_w_o": att_w_o,
            "hgrn_lb_logits": hgrn_lb_logits, "hgrn_w_in": hgrn_w_in, "hgrn_g_norm": hgrn_g_norm,
            "hgrn_w_o": hgrn_w_o,
            "ffn_w_gate": ffn_w_gate, "ffn_w_up": ffn_w_up, "ffn_w_down": ffn_w_down,
            "ln_g": ln_g, "ln_b": ln_b}


def reference(x, positions, att_w_in, att_g_cq, att_w_uq, att_w_iq, att_g_ik, att_b_ik, att_w_o,
              hgrn_lb_logits, hgrn_w_in, hgrn_g_norm, hgrn_w_o,
              ffn_w_gate, ffn_w_up, ffn_w_down, ln_g, ln_b):
    cos_q, sin_q = rope_tables(positions, ATT_ROT_DIM)
    cos_i, sin_i = rope_tables(positions, IDX_ROT_DIM)
    lb_all = jnp.cumsum(jax.nn.softmax(hgrn_lb_logits.astype(jnp.float32), axis=0), axis=0)
    lb_all = lb_all - lb_all[0]
    h = x
    for layer in range(DEPTH):
        j = layer // N_MIXERS
        if layer % N_MIXERS == 0:
            mix = dsa_mixer(h, cos_q, sin_q, cos_i, sin_i, att_w_in[j], att_g_cq[j], att_w_uq[j],
                            att_w_iq[j], att_g_ik[j], att_b_ik[j], att_w_o[j])
        else:
            mix = hgrn2_mixer(h, lb_all[layer], hgrn_w_in[j], hgrn_g_norm[j], hgrn_w_o[j])
        h = layer_norm(DEEPNORM_ALPHA * h + mix, ln_g[layer, 0], ln_b[layer, 0])
        ffn = swiglu(h, ffn_w_gate[layer], ffn_w_up[layer], ffn_w_down[layer])
        h = layer_norm(DEEPNORM_ALPHA * h + ffn, ln_g[layer, 1], ln_b[layer, 1])
    return h
```

```python
from contextlib import ExitStack
import numpy as np
import concourse.bass as bass
import concourse.mybir as mybir
from concourse.bass_utils import run_bass_kernel_spmd

F32 = mybir.dt.float32
BF16 = mybir.dt.bfloat16
I32 = mybir.dt.int32
AF = mybir.ActivationFunctionType
ALU = mybir.AluOpType
AX = mybir.AxisListType

D = 1024
DFF = 2816
NF = DFF // 128
DEPTH = 2
ALPHA = (2 * DEPTH) ** 0.25
LN_EPS = 1e-5
RMS_EPS = 1e-6
IDX_H = 8
IDX_D = 128
ROPE_THETA = 500000.0


class Buf:
    __slots__ = ("name", "w", "r")

    def __init__(self, name):
        self.name = name
        self.w = None
        self.r = []


class Eng:
    def __init__(self, name, e, sem):
        self.name = name
        self.e = e
        self.sem = sem
        self.cnt = 0
        self.seen = {}
        self.ring = []
        self.ring_i = 0


class K:
    def __init__(self, nc, es, ring=8):
        self.nc = nc
        self.es = es
        self.sems = {}
        self.eng = {}
        for name, e in (("pe", nc.tensor), ("dve", nc.vector), ("act", nc.scalar),
                        ("pool", nc.gpsimd), ("sp", nc.sync)):
            sem = es.enter_context(nc.semaphore("s_" + name))
            self.sems[name] = sem
            self.eng[name] = Eng(name, e, sem)
        for q in ("sp", "pool", "act"):
            for j in range(ring):
                key = "d_%s%d" % (q, j)
                self.sems[key] = es.enter_context(nc.semaphore(key))
                self.eng[q].ring.append([key, 0])
        self.nbuf = 0

    def buf(self, name=None):
        self.nbuf += 1
        return Buf(name or ("b%d" % self.nbuf))

    def bufs(self, n, name="b"):
        return [self.buf("%s%d" % (name, i)) for i in range(n)]

    def _deps(self, en, R, W):
        need = {}
        for b in R:
            if b.w is not None:
                k, v, src = b.w
                if src == "pe" and en == "pe":
                    continue
                need[k] = max(need.get(k, 0), v)
        for b in W:
            toks = list(b.r)
            if b.w is not None:
                toks.append(b.w)
            for k, v, src in toks:
                if src == en and en == "pe":
                    continue
                need[k] = max(need.get(k, 0), v)
        E = self.eng[en]
        for k, v in need.items():
            if E.seen.get(k, 0) < v:
                E.e.wait_ge(self.sems[k], v)
                E.seen[k] = v

    def _commit(self, tok, R, W):
        for b in R:
            b.r.append(tok)
        for b in W:
            b.w = tok
            b.r = []

    def op(self, en, fn, R=(), W=()):
        self._deps(en, R, W)
        E = self.eng[en]
        ins = fn(E.e)
        E.cnt += 1
        ins.then_inc(E.sem, 1)
        self._commit((en, E.cnt, en), R, W)

    def dma(self, q, out, in_, R=(), W=(), **kw):
        E = self.eng[q]
        self._deps(q, R, W)
        slot = E.ring[E.ring_i % len(E.ring)]
        E.ring_i += 1
        key = slot[0]
        if slot[1] > 0 and E.seen.get(key, 0) < slot[1]:
            E.e.wait_ge(self.sems[key], slot[1])
            E.seen[key] = slot[1]
        ins = E.e.dma_start(out=out, in_=in_, **kw)
        slot[1] += 16
        ins.then_inc(self.sems[key], 16)
        self._commit((key, slot[1], "dma_" + q), R, W)

    def barrier(self):
        cur = {}
        for name, E in self.eng.items():
            if E.cnt:
                cur[name] = E.cnt
            for key, v in E.ring:
                if v:
                    cur[key] = v
        for name, E in self.eng.items():
            for key, v in cur.items():
                if key == name and name == "pe":
                    continue
                if E.seen.get(key, 0) < v:
                    E.e.wait_ge(self.sems[key], v)
                    E.seen[key] = v

    def finish(self, bufs):
        E = self.eng["sp"]
        for b in bufs:
            if b.w is not None:
                k, v, _ = b.w
                if E.seen.get(k, 0) < v:
                    E.e.wait_ge(self.sems[k], v)
                    E.seen[k] = v


def sb(nc, es, name, shape, dt):
    return es.enter_context(nc.sbuf_tensor("sb_" + name, list(shape), dt))


class Ctx:
    pass


def layer_norm_inplace(k, c, y, yb, gbc, bbc, g_b, tag):
    nc = k.nc
    st = c.ln_stats
    k.op("dve", lambda e: e.bn_stats(out=st[:, 0, :], in_=y[:, 0:512]), R=[yb], W=[c.b_lnst])
    k.op("dve", lambda e: e.bn_stats(out=st[:, 1, :], in_=y[:, 512:1024]), R=[yb], W=[c.b_lnst])
    k.op("dve", lambda e: e.bn_aggr(out=c.ln_mv[:, :], in_=st[:, :, :].rearrange("p a b -> p (a b)")),
         R=[c.b_lnst], W=[c.b_lnmv])
    k.op("dve", lambda e: e.tensor_scalar(out=c.ln_mv[:, 1:2], in0=c.ln_mv[:, 1:2], scalar1=LN_EPS, scalar2=None, op0=ALU.add),
         R=[c.b_lnmv], W=[c.b_lnmv])
    k.op("pool", lambda e: e.tensor_tensor(out=c.ln_rs[:, 0:1], in0=c.ln_mv[:, 1:2], in1=c.mhalf[:, 0:1], op=ALU.pow),
         R=[c.b_lnmv, c.b_mh], W=[c.b_lnrs])
    k.op("dve", lambda e: e.scalar_tensor_tensor(out=c.ln_rs[:, 1:2], in0=c.ln_mv[:, 0:1], scalar=-1.0,
                                                 in1=c.ln_rs[:, 0:1], op0=ALU.mult, op1=ALU.mult),
         R=[c.b_lnmv, c.b_lnrs], W=[c.b_lnnm])
    k.op("act", lambda e: e.activation(out=y, in_=y, func=AF.Identity, bias=c.ln_rs[:, 1:2],
                                       scale=c.ln_rs[:, 0:1]),
         R=[yb, c.b_lnrs, c.b_lnnm], W=[yb])
    k.op("dve", lambda e: e.tensor_tensor(out=y, in0=y, in1=gbc, op=ALU.mult), R=[yb, g_b], W=[yb])
    k.op("pool", lambda e: e.tensor_tensor(out=y, in0=y, in1=bbc, op=ALU.add), R=[yb, g_b], W=[yb])


def load_bcast(k, q, dst, src_row, b):
    k.dma(q, dst, src_row.partition_broadcast(128), W=[b])


def to_feature_major(k, c, src, src_b, dstT, dst_b, col0, ncols=128, nchunks=8, cast_eng="pool"):
    i = c.tp_i
    c.tp_i += 1
    hb, hb_b = c.hb[i % len(c.hb)], c.b_hb[i % len(c.hb)]
    tp, tp_b = c.tp[i % len(c.tp)], c.b_tp[i % len(c.tp)]
    n = nchunks * 128
    if cast_eng == "act":
        k.op("act", lambda e: e.copy(out=hb[:, 0:n], in_=src), R=[src_b], W=[hb_b])
    else:
        k.op(cast_eng, lambda e: e.tensor_copy(out=hb[:, 0:n], in_=src), R=[src_b], W=[hb_b])
    for ch in range(nchunks):
        k.op("pe", lambda e, ch=ch: e.transpose(out=tp[:, ch * 128:(ch + 1) * 128],
                                                in_=hb[:, ch * 128:(ch + 1) * 128], identity=c.ident[:, :]),
             R=[hb_b, c.b_const], W=[tp_b])
    k.op("act", lambda e: e.copy(out=dstT[:, 0:nchunks, col0:col0 + 128],
                                 in_=tp[:, 0:n].rearrange("p (c t) -> p c t", c=nchunks)),
         R=[tp_b], W=[dst_b])


def run_streams(gens):
    acc = [0.0] * len(gens)
    alive = list(range(len(gens)))
    while alive:
        j = min(alive, key=lambda a_: acc[a_])
        try:
            acc[j] += next(gens[j])
        except StopIteration:
            alive.remove(j)


def ffn_pass(k, c, T, src, src_bufs, dst, dst_bufs, wg_d, wu_d, wd_d, lng_d, lnb_d, es_outer, tag):
    nc = k.nc
    with ExitStack() as es:
        wg = sb(nc, es, "wg" + tag, [128, 8, DFF], BF16)
        wu = sb(nc, es, "wu" + tag, [128, 8, DFF], BF16)
        wd = sb(nc, es, "wd" + tag, [128, NF, D], BF16)
        gbc = sb(nc, es, "gbc" + tag, [128, D], F32)
        bbc = sb(nc, es, "bbc" + tag, [128, D], F32)
        NR = 8
        hin = [sb(nc, es, "hin%s%d" % (tag, i), [128, D], F32) for i in range(NR)]
        hT = [sb(nc, es, "hT%s%d" % (tag, i), [128, 8, 512], BF16) for i in range(1)]
        actT = sb(nc, es, "actT" + tag, [128, NF, 512], BF16)
        sg = [sb(nc, es, "sg%s%d" % (tag, i), [128, 512], BF16) for i in range(2)]
        NFG = NF // 2
        b_wg = k.bufs(NFG, "wg" + tag)
        b_wu = k.bufs(NFG, "wu" + tag)
        b_wd = k.bufs(NFG, "wd" + tag)
        b_gb = k.buf()
        b_hin = k.bufs(NR, "hin" + tag)
        b_hT = k.bufs(1)
        b_act = k.bufs(NF, "act" + tag)
        b_sg = k.bufs(2)
        ntile = T // 128
        ngrp = T // 512

        def load_tile(ti):
            r = ti % NR
            k.dma("sp", hin[r][:, :], src[ti * 128:(ti + 1) * 128, :], R=[src_bufs[ti]], W=[b_hin[r]])

        for ti in range(min(8, ntile)):
            load_tile(ti)
        load_bcast(k, "sp", gbc[:, :], lng_d, b_gb)
        load_bcast(k, "sp", bbc[:, :], lnb_d, b_gb)
        wg3 = wg_d.rearrange("p (c f) -> p c f", c=8)
        wu3 = wu_d.rearrange("p (c f) -> p c f", c=8)
        for fg in range(NFG):
            cs = slice(fg * 256, (fg + 1) * 256)
            k.dma("pool", wg[:, :, cs], wg3[:, :, cs], W=[b_wg[fg]])
            k.dma("pool", wu[:, :, cs], wu3[:, :, cs], W=[b_wu[fg]])
        for fg in range(NFG):
            k.dma("pool", wd[:, 2 * fg:2 * fg + 2, :], wd_d[:, 2 * fg * D:(2 * fg + 2) * D].rearrange("p (f d) -> p f d", f=2),
                  W=[b_wd[fg]])
        ps = [es.enter_context(nc.psum_tensor("ps%s%d" % (tag, i), [128, 512], F32)) for i in range(6)]
        bps = k.bufs(6, "ps")
        c.tp = [es.enter_context(nc.psum_tensor("tp%s%d" % (tag, i), [128, 1024], BF16)) for i in range(2)]
        c.b_tp = k.bufs(2, "tp")

        def feat(g):
            for t in range(4):
                ti = g * 4 + t
                to_feature_major(k, c, hin[ti % NR][:, :], b_hin[ti % NR], hT[0], b_hT[0], t * 128, cast_eng="act")

        feat(0)
        for g in range(ngrp):
            hTg, b_hTg = hT[0], b_hT[0]
            for f in range(NF):
                pg, bg = ps[f % 2], bps[f % 2]
                pu, bu = ps[2 + f % 2], bps[2 + f % 2]
                for kc in range(8):
                    k.op("pe", lambda e, kc=kc: e.matmul(pg[:, :], wg[:, kc, f * 128:(f + 1) * 128], hTg[:, kc, :],
                                                         start=(kc == 0), stop=(kc == 7)),
                         R=[b_wg[f // 2], b_hTg], W=[bg])
                for kc in range(8):
                    k.op("pe", lambda e, kc=kc: e.matmul(pu[:, :], wu[:, kc, f * 128:(f + 1) * 128], hTg[:, kc, :],
                                                         start=(kc == 0), stop=(kc == 7)),
                         R=[b_wu[f // 2], b_hTg], W=[bu])
                s_, bs = sg[f % 2], b_sg[f % 2]
                k.op("act", lambda e: e.activation(out=s_[:, :], in_=pg[:, :], func=AF.Silu), R=[bg], W=[bs])
                k.op("dve", lambda e: e.tensor_tensor(out=actT[:, f, :], in0=s_[:, :], in1=pu[:, :], op=ALU.mult),
                     R=[bs, bu], W=[b_act[f]])
            if g + 1 < ngrp:
                feat(g + 1)
            for t in range(4):
                ti = g * 4 + t
                r = ti % NR
                for half in range(2):
                    py, by = ps[4 + half], bps[4 + half]
                    for f in range(NF):
                        k.op("pe", lambda e, f=f: e.matmul(py[:, :], actT[:, f, t * 128:(t + 1) * 128],
                                                           wd[:, f, half * 512:(half + 1) * 512],
                                                           start=(f == 0), stop=(f == NF - 1)),
                             R=[b_act[f], b_wd[f // 2]], W=[by])
                    k.op("dve", lambda e: e.scalar_tensor_tensor(
                        out=hin[r][:, half * 512:(half + 1) * 512], in0=hin[r][:, half * 512:(half + 1) * 512],
                        scalar=ALPHA, in1=py[:, :], op0=ALU.mult, op1=ALU.add), R=[b_hin[r], by], W=[b_hin[r]])
                layer_norm_inplace(k, c, hin[r][:, :], b_hin[r], gbc[:, :], bbc[:, :], b_gb, tag)
                k.dma("sp", dst[ti * 128:(ti + 1) * 128, :], hin[r][:, :], R=[b_hin[r]], W=[dst_bufs[ti]])
                nxt = ti + 8
                if nxt < ntile:
                    load_tile(nxt)


def hgrn_pass(k, c, T, src, src_bufs, dst, dst_bufs, w, lng_d, lnb_d, tag):
    nc = k.nc
    LN2 = float(np.log(2.0))
    with ExitStack() as es:
        w_in = sb(nc, es, "hw_in_sb", [128, 8, 4096], BF16)
        w_o = sb(nc, es, "hw_o_sb", [128, 8, D], BF16)
        A_bc = sb(nc, es, "hA", [128, D], F32)
        B_bc = sb(nc, es, "hB", [128, D], F32)
        gn_bc = sb(nc, es, "hgn", [128, D], F32)
        gbc = sb(nc, es, "gbc" + tag, [128, D], F32)
        bbc = sb(nc, es, "bbc" + tag, [128, D], F32)
        U = [sb(nc, es, "hU%d" % i, [128, 128], F32) for i in range(4)]
        nln2 = sb(nc, es, "hnln2", [128, 1], F32)
        mtri = sb(nc, es, "hmtri", [128, 128], BF16)
        ones = sb(nc, es, "hones", [128, 1], F32)
        NR = 3
        hin = [sb(nc, es, "hhin%d" % i, [128, D], F32) for i in range(NR)]
        xT = sb(nc, es, "hxT", [128, 8, 128], BF16)
        th = [sb(nc, es, "hth%d" % i, [128, 512], F32) for i in range(2)]
        q2 = sb(nc, es, "hq2", [128, D], F32)
        fg = sb(nc, es, "hfg", [128, D], F32)
        kk = sb(nc, es, "hkk", [128, D], F32)
        logf = sb(nc, es, "hlogf", [128, D], F32)
        et = [sb(nc, es, "het%d" % i, [128, D], F32) for i in range(2)]
        qh = sb(nc, es, "hqh", [128, D], BF16)
        qi = sb(nc, es, "hqi", [128, D], BF16)
        kmix = sb(nc, es, "hkmix", [128, D], BF16)
        kta = sb(nc, es, "hkta", [128, D], BF16)
        gg = [sb(nc, es, "hgg%d" % i, [128, D], F32) for i in range(2)]
        vb = [sb(nc, es, "hvb%d" % i, [128, D], BF16) for i in range(2)]
        kh = [sb(nc, es, "hkh%d" % i, [128, D], BF16) for i in range(2)]
        qhT = [sb(nc, es, "hqhT%d" % i, [128, 8, 128], BF16) for i in range(2)]
        qiT = [sb(nc, es, "hqiT%d" % i, [128, 8, 128], BF16) for i in range(2)]
        kmixT = [sb(nc, es, "hkmixT%d" % i, [128, 8, 128], BF16) for i in range(2)]
        ktaT = [sb(nc, es, "hktaT%d" % i, [128, 8, 128], BF16) for i in range(2)]
        dec = [sb(nc, es, "hdec%d" % i, [128, 8], F32) for i in range(2)]
        attn = sb(nc, es, "hattn", [128, 8, 128], BF16)
        S = sb(nc, es, "hS", [128, 8, 128], F32)
        Sb = sb(nc, es, "hSb", [128, 8, 128], BF16)
        sq = sb(nc, es, "hsq", [128, 128], BF16)
        ssum = sb(nc, es, "hssum", [128, 8], F32)
        rstd = sb(nc, es, "hrstd", [128, 8], F32)
        on = sb(nc, es, "hon", [128, 8, 128], F32)
        ogb = sb(nc, es, "hogb", [128, D], BF16)
        ogT = sb(nc, es, "hogT", [128, 8, 128], BF16)
        Ab = [es.enter_context(nc.psum_tensor("hA%d" % i, [128, 512], F32)) for i in range(2)]
        b_A = k.bufs(2, "hAb")
        c.tp = [es.enter_context(nc.psum_tensor("htp", [128, 1024], BF16))]
        c.b_tp = k.bufs(1, "htp")
        misc = es.enter_context(nc.psum_tensor("hmisc", [128, 512], F32))
        b_misc = k.buf()
        Yb = [es.enter_context(nc.psum_tensor("hY%d" % i, [128, 512], F32)) for i in range(4)]
        b_Y = k.bufs(4, "hY")
        b_w, b_wo, b_cst, b_gb = k.bufs(4, "hw")
        b_hin = k.bufs(NR, "hhin")
        b_xT = k.buf()
        b_th = k.bufs(2)
        (b_q2, b_fg, b_kk, b_logf, b_qh, b_qi, b_kmix, b_kta, b_attn, b_S, b_Sb, b_sq, b_ssum, b_rstd, b_on,
         b_ogb, b_ogT) = k.bufs(17, "hh")
        b_et = k.bufs(2)
        b_gg, b_vb, b_kh, b_qhT, b_qiT, b_kmixT, b_ktaT, b_dec = [k.bufs(2, "hx%d" % j) for j in range(8)]

        ntile = T // 128

        def load_tile(ti):
            r = ti % NR
            k.dma("sp", hin[r][:, :], src[ti * 128:(ti + 1) * 128, :], R=[src_bufs[ti]], W=[b_hin[r]])

        for ti in range(min(NR, ntile)):
            load_tile(ti)
        b_wj = k.bufs(8, "hwj")
        w3 = w["hw_in"].rearrange("p (c f) -> p c f", c=8)
        for j in range(8):
            k.dma("pool", w_in[:, :, j * 512:(j + 1) * 512], w3[:, :, j * 512:(j + 1) * 512], W=[b_wj[j]])
        for kc in range(0, 8, 2):
            k.dma("pool", w_o[:, kc:kc + 2, :], w["hw_o"][:, kc * D:(kc + 2) * D].rearrange("p (a d) -> p a d", a=2),
                  W=[b_wo])
        for i in range(4):
            k.dma("sp", U[i][:, :], w["hU"][i, :, :], W=[b_cst])
        k.op("dve", lambda e: e.memset(nln2[:, :], -LN2), W=[b_cst])
        k.dma("pool", mtri[:, :], w["hmtri"][:, :], W=[b_cst])
        k.dma("sp", ones[:, :], w["hones"][:, :], W=[b_cst])
        load_bcast(k, "sp", gbc[:, :], lng_d, b_gb)
        load_bcast(k, "sp", bbc[:, :], lnb_d, b_gb)
        load_bcast(k, "sp", gn_bc[:, :], w["hgn"][0:1, :], b_gb)
        load_bcast(k, "sp", A_bc[:, :], w["hlbl"][0:1, :], b_cst)
        load_bcast(k, "sp", B_bc[:, :], w["hlbl"][1:2, :], b_cst)
        k.op("dve", lambda e: e.tensor_tensor(out=B_bc[:, :], in0=B_bc[:, :], in1=A_bc[:, :], op=ALU.subtract),
             R=[b_cst], W=[b_cst])
        k.op("act", lambda e: e.activation(out=B_bc[:, :], in_=B_bc[:, :], func=AF.Tanh, scale=0.5), R=[b_cst], W=[b_cst])
        k.op("dve", lambda e: e.tensor_scalar(out=A_bc[:, :], in0=B_bc[:, :], scalar1=0.25, scalar2=0.75,
                                              op0=ALU.mult, op1=ALU.add), R=[b_cst], W=[b_cst])
        k.op("dve", lambda e: e.tensor_scalar(out=B_bc[:, :], in0=B_bc[:, :], scalar1=-0.25, scalar2=0.25,
                                              op0=ALU.mult, op1=ALU.add), R=[b_cst], W=[b_cst])
        k.op("dve", lambda e: e.memset(S[:, :, :], 0.0), W=[b_S])
        k.op("pool", lambda e: e.memset(Sb[:, :, :], 0.0), W=[b_Sb])
        k.op("pool", lambda e: e.memset(kta[:, :], 0.0), W=[b_kta])
        thi_box = [0]
        hfill_src = sb(nc, es, "hfill", [128, 256], BF16)
        k.op("pool", lambda e: e.memset(hfill_src[:, :], 1.0), W=[c.b_fill])

        def hfiller(n=1):
            for _ in range(n):
                k.op("pe", lambda e: e.matmul(misc[:, 256:512], c.ident[:, :], hfill_src[:, :], start=True, stop=True),
                     R=[c.b_const, c.b_fill], W=[c.b_fillps])

        def gen_X(ti):
            r = ti % NR
            p_ = ti % 2
            to_feature_major(k, c, hin[r][:, :], b_hin[r], xT, b_xT, 0, cast_eng="act")
            yield 2.0
            for j in range(8):
                p, bp = Ab[j % 2], b_A[j % 2]
                for kc in range(8):
                    k.op("pe", lambda e, kc=kc: e.matmul(p[:, :], xT[:, kc, :], w_in[:, kc, j * 512:(j + 1) * 512],
                                                         start=(kc == 0), stop=(kc == 7)), R=[b_xT, b_wj[j]], W=[bp])
                hfiller(FILL_H)
                cs = slice((j % 2) * 512, (j % 2) * 512 + 512)
                if j in (0, 1, 2, 3, 6, 7):
                    t_, bt_ = th[thi_box[0] % 2], b_th[thi_box[0] % 2]
                    thi_box[0] += 1
                    k.op("act", lambda e: e.activation(out=t_[:, :], in_=p[:, :], func=AF.Tanh, scale=0.5),
                         R=[bp], W=[bt_])
                if j in (0, 1):
                    k.op("dve", lambda e: e.scalar_tensor_tensor(out=q2[:, cs], in0=t_[:, :], scalar=1.0, in1=p[:, :],
                                                                 op0=ALU.add, op1=ALU.mult), R=[bt_, bp], W=[b_q2])
                elif j in (2, 3):
                    k.op("dve", lambda e: e.tensor_tensor(out=fg[:, cs], in0=t_[:, :], in1=B_bc[:, cs], op=ALU.mult),
                         R=[bt_, b_cst], W=[b_fg])
                    k.op("dve", lambda e: e.tensor_tensor(out=fg[:, cs], in0=fg[:, cs], in1=A_bc[:, cs], op=ALU.add),
                         R=[b_fg, b_cst], W=[b_fg])
                elif j in (4, 5):
                    k.op("act", lambda e: e.copy(out=vb[p_][:, cs], in_=p[:, :]), R=[bp], W=[b_vb[p_]])
                else:
                    k.op("dve", lambda e: e.scalar_tensor_tensor(out=gg[p_][:, cs], in0=t_[:, :], scalar=1.0, in1=p[:, :],
                                                                 op0=ALU.add, op1=ALU.mult), R=[bt_, bp], W=[b_gg[p_]])
                yield 2.0
            k.op("act", lambda e: e.activation(out=logf[:, :], in_=fg[:, :], func=AF.Ln), R=[b_fg], W=[b_logf])
            k.op("dve", lambda e: e.tensor_scalar(out=kk[:, :], in0=fg[:, :], scalar1=-1.0, scalar2=1.0,
                                                   op0=ALU.mult, op1=ALU.add), R=[b_fg], W=[b_kk])
            yield 2.0

            def cums(m):
                for hf in range(2):
                    p, bp = Ab[hf], b_A[hf]
                    k.op("pe", lambda e, m=m, hf=hf, p=p: e.matmul(p[:, :], U[m][:, :], logf[:, hf * 512:(hf + 1) * 512],
                                                                    start=True, stop=True), R=[b_logf, b_cst], W=[bp])
            for h in range(8):
                k.op("pe", lambda e, h=h: e.matmul(misc[:, h:h + 1], logf[:, h * 128:(h + 1) * 128], ones[:, :],
                                                   start=True, stop=True), R=[b_logf, b_cst], W=[b_misc])
            k.op("act", lambda e: e.activation(out=dec[p_][:, :], in_=misc[:, 0:8], func=AF.Exp), R=[b_misc], W=[b_dec[p_]])
            e0, be0 = et[0], b_et[0]
            e1, be1 = et[1], b_et[1]
            cums(0)
            for hf in range(2):
                cs = slice(hf * 512, hf * 512 + 512)
                k.op("act", lambda e, cs=cs, hf=hf: e.activation(out=e0[:, cs], in_=Ab[hf][:, :], func=AF.Exp, bias=nln2[:, 0:1]),
                     R=[b_A[hf], b_cst], W=[be0])
                k.op("dve", lambda e, cs=cs: e.tensor_tensor(out=qh[:, cs], in0=q2[:, cs], in1=e0[:, cs], op=ALU.mult),
                     R=[b_q2, be0], W=[b_qh])
                k.op("act", lambda e, cs=cs, hf=hf: e.activation(out=e1[0:64, cs], in_=Ab[hf][0:64, :], func=AF.Exp, scale=-1.0),
                     R=[b_A[hf]], W=[be1])
                k.op("dve", lambda e, cs=cs: e.tensor_tensor(out=kta[0:64, cs], in0=kk[0:64, cs], in1=e1[0:64, cs], op=ALU.mult),
                     R=[b_kk, be1], W=[b_kta])
            yield 4.0
            cums(3)
            for hf in range(2):
                cs = slice(hf * 512, hf * 512 + 512)
                k.op("act", lambda e, cs=cs, hf=hf: e.activation(out=e0[:, cs], in_=Ab[hf][:, :], func=AF.Exp, bias=nln2[:, 0:1]),
                     R=[b_A[hf], b_cst], W=[be0])
                k.op("dve", lambda e, cs=cs: e.tensor_tensor(out=qi[:, cs], in0=q2[:, cs], in1=e0[:, cs], op=ALU.mult),
                     R=[b_q2, be0], W=[b_qi])
            yield 3.0
            cums(1)
            for hf in range(2):
                cs = slice(hf * 512, hf * 512 + 512)
                k.op("act", lambda e, cs=cs, hf=hf: e.activation(out=e1[:, cs], in_=Ab[hf][:, :], func=AF.Exp),
                     R=[b_A[hf]], W=[be1])
                k.op("dve", lambda e, cs=cs: e.tensor_tensor(out=kh[p_][:, cs], in0=kk[:, cs], in1=e1[:, cs], op=ALU.mult),
                     R=[b_kk, be1], W=[b_kh[p_]])
            yield 3.0
            cums(2)
            for hf in range(2):
                cs = slice(hf * 512, hf * 512 + 512)
                k.op("act", lambda e, cs=cs, hf=hf: e.activation(out=e0[:, cs], in_=Ab[hf][:, :], func=AF.Exp),
                     R=[b_A[hf]], W=[be0])
                k.op("dve", lambda e, cs=cs: e.tensor_tensor(out=kmix[:, cs], in0=kk[:, cs], in1=e0[:, cs], op=ALU.mult),
                     R=[b_kk, be0], W=[b_kmix])
            yield 3.0
            tgts = [(c.tp[0][:, :], c.b_tp[0]), (Ab[0][:, :].bitcast(BF16), b_A[0]), (Ab[1][:, :].bitcast(BF16), b_A[1]),
                    (c.tp[0][:, :], c.b_tp[0])]
            for ri, (srcT, bsrc, dT, bd) in enumerate(((qh, b_qh, qhT[p_], b_qhT[p_]), (qi, b_qi, qiT[p_], b_qiT[p_]),
                                                       (kta, b_kta, ktaT[p_], b_ktaT[p_]), (kmix, b_kmix, kmixT[p_], b_kmixT[p_]))):
                tp, tp_b = tgts[ri]
                for h in range(8):
                    k.op("pe", lambda e, h=h, srcT=srcT: e.transpose(out=tp[:, h * 128:(h + 1) * 128],
                                                                      in_=srcT[:, h * 128:(h + 1) * 128],
                                                                      identity=c.ident[:, :]),
                         R=[bsrc, c.b_const], W=[tp_b])
                k.op("act", lambda e, dT=dT: e.copy(out=dT[:, :, :], in_=tp[:, :].rearrange("p (h t) -> p h t", h=8)),
                     R=[tp_b], W=[bd])
                yield 1.5

        def gen_Y(ti):
            r = ti % NR
            p_ = ti % 2
            for h in range(8):
                p, bp = Yb[h // 4], b_Y[h // 4]
                c0 = (h % 4) * 128
                k.op("pe", lambda e, h=h, p=p, c0=c0: e.matmul(p[:, c0:c0 + 64], ktaT[p_][:, h, :], qhT[p_][:, h, 0:64],
                                                                start=True, stop=True),
                     R=[b_ktaT[p_], b_qhT[p_]], W=[bp])
                k.op("pe", lambda e, h=h, p=p, c0=c0: e.matmul(p[:, c0 + 64:c0 + 128], kmixT[p_][:, h, :], qhT[p_][:, h, 64:128],
                                                                start=True, stop=True),
                     R=[b_kmixT[p_], b_qhT[p_]], W=[bp])
            for q4 in range(2):
                k.op("dve", lambda e, q4=q4: e.tensor_tensor(
                    out=attn[:, q4 * 4:(q4 + 1) * 4, :], in0=Yb[q4][:, :].rearrange("p (h t) -> p h t", h=4),
                    in1=mtri[:, :].unsqueeze(1).to_broadcast([128, 4, 128]), op=ALU.mult),
                    R=[b_Y[q4], b_cst], W=[b_attn])
            yield 3.0
            for h in range(8):
                p, bp = Yb[2 + h // 4], b_Y[2 + h // 4]
                c0 = (h % 4) * 128
                k.op("pe", lambda e, h=h, p=p, c0=c0: e.matmul(p[:, c0:c0 + 128], attn[:, h, :], vb[p_][:, h * 128:(h + 1) * 128],
                                                                start=True, stop=False), R=[b_attn, b_vb[p_]], W=[bp])
                k.op("pe", lambda e, h=h, p=p, c0=c0: e.matmul(p[:, c0:c0 + 128], qiT[p_][:, h, :], Sb[:, h, :],
                                                                start=False, stop=True), R=[b_qiT[p_], b_Sb], W=[bp])
            yield 2.0
            for h in range(8):
                p, bp = Yb[h // 4], b_Y[h // 4]
                c0 = (h % 4) * 128
                k.op("pe", lambda e, h=h, p=p, c0=c0: e.matmul(p[:, c0:c0 + 128], kh[p_][:, h * 128:(h + 1) * 128],
                                                                vb[p_][:, h * 128:(h + 1) * 128], start=True, stop=True),
                     R=[b_kh[p_], b_vb[p_]], W=[bp])
            for h in range(8):
                p, bp = Yb[h // 4], b_Y[h // 4]
                c0 = (h % 4) * 128
                k.op("dve", lambda e, h=h, p=p, c0=c0: e.scalar_tensor_tensor(
                    out=S[:, h, :], in0=S[:, h, :], scalar=dec[p_][:, h:h + 1], in1=p[:, c0:c0 + 128],
                    op0=ALU.mult, op1=ALU.add), R=[b_S, b_dec[p_], bp], W=[b_S])
            k.op("pool", lambda e: e.tensor_copy(out=Sb[:, :, :], in_=S[:, :, :]), R=[b_S], W=[b_Sb])
            yield 4.0
            for h in range(8):
                k.op("act", lambda e, h=h: e.activation(out=sq[:, :], in_=Yb[2 + h // 4][:, (h % 4) * 128:(h % 4) * 128 + 128],
                                                        func=AF.Square, accum_out=ssum[:, h:h + 1]),
                     R=[b_Y[2 + h // 4]], W=[b_sq, b_ssum])
            k.op("dve", lambda e: e.tensor_scalar(out=ssum[:, :], in0=ssum[:, :], scalar1=4.0 / 128, scalar2=4.0 * RMS_EPS,
                                                  op0=ALU.mult, op1=ALU.add), R=[b_ssum], W=[b_ssum])
            k.op("pool", lambda e: e.tensor_tensor(out=rstd[:, :], in0=ssum[:, :], in1=c.mhalf[:, 0:8], op=ALU.pow),
                 R=[b_ssum, c.b_mh], W=[b_rstd])
            for hf in range(2):
                k.op("dve", lambda e, hf=hf: e.tensor_tensor(
                    out=on[:, hf * 4:(hf + 1) * 4, :], in0=Yb[2 + hf][:, :].rearrange("p (h v) -> p h v", h=4),
                    in1=rstd[:, hf * 4:(hf + 1) * 4].unsqueeze(2).to_broadcast([128, 4, 128]), op=ALU.mult),
                    R=[b_Y[2 + hf], b_rstd], W=[b_on])
            onf = on[:, :, :].rearrange("p h v -> p (h v)")
            k.op("dve", lambda e: e.tensor_tensor(out=onf, in0=onf, in1=gn_bc[:, :], op=ALU.mult), R=[b_on, b_gb], W=[b_on])
            k.op("dve", lambda e: e.tensor_tensor(out=ogb[:, :], in0=onf, in1=gg[p_][:, :], op=ALU.mult), R=[b_on, b_gg[p_]], W=[b_ogb])
            yield 4.0
            ytp = Yb[2][:, :].bitcast(BF16)
            for ch in range(8):
                k.op("pe", lambda e, ch=ch: e.transpose(out=ytp[:, ch * 128:(ch + 1) * 128], in_=ogb[:, ch * 128:(ch + 1) * 128],
                                                        identity=c.ident[:, :]), R=[b_ogb, c.b_const], W=[b_Y[2]])
            k.op("act", lambda e: e.copy(out=ogT[:, :, :], in_=ytp[:, :].rearrange("p (c t) -> p c t", c=8)), R=[b_Y[2]], W=[b_ogT])
            yield 2.0
            for half in range(2):
                p, bp = Yb[half], b_Y[half]
                for kc in range(8):
                    k.op("pe", lambda e, kc=kc, p=p: e.matmul(p[:, :], ogT[:, kc, :], w_o[:, kc, half * 512:(half + 1) * 512],
                                                               start=(kc == 0), stop=(kc == 7)), R=[b_ogT, b_wo], W=[bp])
                k.op("dve", lambda e, p=p: e.scalar_tensor_tensor(
                    out=hin[r][:, half * 512:(half + 1) * 512], in0=hin[r][:, half * 512:(half + 1) * 512],
                    scalar=ALPHA, in1=p[:, :], op0=ALU.mult, op1=ALU.add), R=[b_hin[r], bp], W=[b_hin[r]])
                yield 2.0
            layer_norm_inplace(k, c, hin[r][:, :], b_hin[r], gbc[:, :], bbc[:, :], b_gb, tag)
            k.dma("sp", dst[ti * 128:(ti + 1) * 128, :], hin[r][:, :], R=[b_hin[r]], W=[dst_bufs[ti]])
            if ti + NR < ntile:
                load_tile(ti + NR)
            yield 5.0

        run_streams([gen_X(0)])
        for ti in range(ntile):
            gens = [gen_Y(ti)]
            if ti + 1 < ntile:
                gens.append(gen_X(ti + 1))
            run_streams(gens)


NIT = 13
FILL_N = 512
FILL_IDX = True
FILL_ATT = 0
FILL_H = 2
MASK_NEG = -30000.0


def rope_inplace(k, xt, b_x, H, Dh, half, cos, sin, b_tab, tc_, ts_, b_tc, b_ts):
    xv = xt.rearrange("p (h d) -> p h d", h=H)[:, :, 0:2 * half].rearrange("p h (two f) -> p h two f", two=2)
    tcv = tc_[:, 0:H * 2 * half].rearrange("p (h two f) -> p h two f", h=H, two=2)
    tsv = ts_[:, 0:H * 2 * half].rearrange("p (h two f) -> p h two f", h=H, two=2)
    cb = cos.unsqueeze(1).unsqueeze(1).to_broadcast([128, H, 2, half])
    sbb = sin.unsqueeze(1).unsqueeze(1).to_broadcast([128, H, 2, half])
    k.op("dve", lambda e: e.tensor_tensor(out=tcv, in0=xv, in1=cb, op=ALU.mult), R=[b_x, b_tab], W=[b_tc])
    k.op("pool", lambda e: e.tensor_tensor(out=tsv, in0=xv, in1=sbb, op=ALU.mult), R=[b_x, b_tab], W=[b_ts])
    k.op("dve", lambda e: e.tensor_tensor(out=xv[:, :, 0, :], in0=tcv[:, :, 0, :], in1=tsv[:, :, 1, :], op=ALU.subtract),
         R=[b_tc, b_ts], W=[b_x])
    k.op("pool", lambda e: e.tensor_tensor(out=xv[:, :, 1, :], in0=tcv[:, :, 1, :], in1=tsv[:, :, 0, :], op=ALU.add),
         R=[b_tc, b_ts], W=[b_x])


def dsa_pass(k, c, T, src, src_bufs, dst, dst_bufs, w, lng_d, lnb_d, tag):
    nc = k.nc
    ntile = T // 128
    topk = min(256, T // 4)
    PI = float(np.pi)
    with ExitStack() as es:
        w_in = sb(nc, es, "aw_in", [128, 8, 648], BF16)
        w_uq = sb(nc, es, "aw_uq", [128, 2, D], BF16)
        w_iq = sb(nc, es, "aw_iq", [128, 2, D], BF16)
        w_o = sb(nc, es, "aw_o", [128, 8, D], BF16)
        gbc = sb(nc, es, "gbc" + tag, [128, D], F32)
        bbc = sb(nc, es, "bbc" + tag, [128, D], F32)
        gcq = sb(nc, es, "agcq", [128, 256], F32)
        gik = sb(nc, es, "agik", [128, 128], F32)
        bik = sb(nc, es, "abik", [128, 128], F32)
        dmask = sb(nc, es, "admask", [128, 128], F32)
        identf = sb(nc, es, "aidentf", [128, 128], F32)
        ones_f = sb(nc, es, "aones", [128, 4], F32)
        pow2 = sb(nc, es, "apow2", [128, NIT + 1], F32)
        invq = sb(nc, es, "ainvq", [128, 8], F32)
        invi = sb(nc, es, "ainvi", [128, 16], F32)
        posi = sb(nc, es, "aposi", [128, ntile], I32)
        posf = sb(nc, es, "aposf", [128, ntile], F32)
        cosq = sb(nc, es, "acosq", [128, ntile, 8], F32)
        sinq = sb(nc, es, "asinq", [128, ntile, 8], F32)
        cosi = sb(nc, es, "acosi", [128, ntile, 16], F32)
        sini = sb(nc, es, "asini", [128, ntile, 16], F32)
        b_rr = k.buf("arr")
        kT = sb(nc, es, "akT", [128, T], BF16)
        kidxT = sb(nc, es, "akidxT", [128, T], BF16)
        vaug0 = sb(nc, es, "avaug0", [128, ntile, 65], BF16)
        vaug1 = sb(nc, es, "avaug1", [128, ntile, 128], BF16)
        score2 = [sb(nc, es, "ascore%d" % j, [128, T], F32) for j in range(2)]
        amax = [sb(nc, es, "aamax%d" % j, [128, 1], F32) for j in range(2)]
        maskb = sb(nc, es, "amaskb", [128, T], BF16)
        junk = maskb
        maskT = [sb(nc, es, "amaskT%d" % j, [128, ntile, 128], BF16) for j in range(2)]
        NR = 5
        hin = [sb(nc, es, "ahin%d" % j, [128, D], F32) for j in range(NR)]
        xT = [sb(nc, es, "axT%d" % j, [128, 8, 128], BF16) for j in range(2)]
        cqn = sb(nc, es, "acqn", [128, 256], BF16)
        cqT = sb(nc, es, "acqT", [128, 2, 128], BF16)
        qf = sb(nc, es, "aqf", [128, D], F32)
        qif = sb(nc, es, "aqif", [128, D], F32)
        qb = sb(nc, es, "aqb", [128, D], BF16)
        qib = sb(nc, es, "aqib", [128, D], BF16)
        qT = [sb(nc, es, "aqT%d" % j, [128, 8, 128], BF16) for j in range(3)]
        qidxT = sb(nc, es, "aqidxT", [128, 8, 128], BF16)
        kf = sb(nc, es, "akf", [128, 128], F32)
        kif = sb(nc, es, "akif", [128, 128], F32)
        kb_ = sb(nc, es, "akb", [128, 256], BF16)
        tcs = sb(nc, es, "atcs", [128, 256], F32)
        tss = sb(nc, es, "atss", [128, 256], F32)
        sm = sb(nc, es, "asm", [128, 16], F32)
        lst = sb(nc, es, "alst", [128, 6], F32)
        lmv = sb(nc, es, "almv", [128, 2], F32)
        lrs = sb(nc, es, "alrs", [128, 2], F32)
        wabs = sb(nc, es, "awabs", [128, 8], F32)
        sgn = sb(nc, es, "asgn", [128, 8], F32)
        dsg = sb(nc, es, "adsg", [128, 8, 128], BF16)
        rsb = [sb(nc, es, "arsb%d" % j, [128, 512], BF16) for j in range(2)]
        esb = [sb(nc, es, "aesb%d" % j, [128, 4, 128], BF16) for j in range(4)]
        oT = sb(nc, es, "aoT", [128, 4, 512], F32)
        rden = sb(nc, es, "arden", [128, 16], F32)
        onrm = sb(nc, es, "aonrm", [128, 8, 128], BF16)
        steps = sb(nc, es, "asteps", [128, NIT + 1], F32)
        bis = sb(nc, es, "abis", [128, 8], F32)
        G = [es.enter_context(nc.psum_tensor("aG%d" % j, [128, 512], F32)) for j in range(2)]
        b_G = k.bufs(2, "aG")
        sc = es.enter_context(nc.psum_tensor("asc", [128, 512], F32))
        b_sc = k.buf()
        st2_t = es.enter_context(nc.psum_tensor("ast2", [128, 512], F32))
        c.tp = [G[0][:, :].bitcast(BF16)]
        c.b_tp = [b_G[0]]
        st = [es.enter_context(nc.psum_tensor("ast%d" % j, [128, 512], F32)) for j in range(2)]
        b_st = k.bufs(2, "ast")
        st3 = [st[0], st[1], st2_t]
        b_st3 = [b_st[0], b_st[1], k.buf("ast2")]
        ot = [es.enter_context(nc.psum_tensor("aot%d" % j, [128, 512], F32)) for j in range(2)]
        b_ot = k.bufs(2, "aot")
        G_q = st[0]
        b_Gq = b_st[0]
        fill_ps = ot[1]
        fill_src = sb(nc, es, "afill", [128, 512], BF16)
        k.op("pool", lambda e: e.memset(fill_src[:, :], 1.0), W=[c.b_fill])

        def filler(n=512):
            k.op("pe", lambda e: e.matmul(fill_ps[:, 0:n], c.ident[:, :], fill_src[:, 0:n], start=True, stop=True),
                 R=[c.b_const, c.b_fill], W=[c.b_fillps])
        (b_w, b_cst, b_gb, b_tab, b_kT, b_kidxT, b_vaug, b_score, b_junk, b_maskb, b_cqn, b_cqT,
         b_qf, b_qif, b_qb, b_qib, b_qidxT, b_kf, b_kif, b_kb, b_tc, b_ts, b_sm, b_wabs, b_sgn, b_dsg,
         b_oT, b_rden, b_onrm, b_steps, b_bis, b_mid, b_cnt, b_t, b_lst, b_lmv, b_lrs, b_lnm) = k.bufs(38, "aa")
        b_maskT = k.bufs(2, "amT")
        b_qT = k.bufs(3, "aqT")
        b_score2 = k.bufs(2, "asc2")
        b_amax = k.bufs(2, "aamx")
        b_hin = k.bufs(NR, "ahin")
        b_xT = k.bufs(2)
        b_rsb = k.bufs(2)
        b_esb = k.bufs(4)

        for kc in range(8):
            k.dma("pool", w_in[:, kc, :], w["aw_in"][:, kc * 648:(kc + 1) * 648], W=[b_w])
        k.dma("pool", w_uq[:, :, :], w["aw_uq"][:, :].rearrange("p (a d) -> p a d", a=2), W=[b_w])
        k.dma("pool", w_iq[:, :, :], w["aw_iq"][:, :].rearrange("p (a d) -> p a d", a=2), W=[b_w])
        for h0 in range(0, 8, 4):
            k.dma("pool", w_o[:, h0:h0 + 4, :], w["aw_o"][:, h0 * D:(h0 + 4) * D].rearrange("p (a d) -> p a d", a=4), W=[b_w])
        load_bcast(k, "sp", gbc[:, :], lng_d, b_gb)
        load_bcast(k, "sp", bbc[:, :], lnb_d, b_gb)
        load_bcast(k, "sp", gcq[:, :], w["agcq"][0:1, :], b_cst)
        load_bcast(k, "sp", gik[:, :], w["agik"][0:1, :], b_cst)
        load_bcast(k, "sp", bik[:, :], w["abik"][0:1, :], b_cst)
        load_bcast(k, "sp", invq[:, :], w["ainvq"][0:1, :], b_cst)
        load_bcast(k, "sp", invi[:, :], w["ainvi"][0:1, :], b_cst)
        load_bcast(k, "sp", pow2[:, :], w["apow2"][0:1, :], b_cst)
        k.dma("sp", dmask[:, :], w["admask"][:, :], W=[b_cst])
        k.dma("sp", identf[:, :], w["aidentf"][:, :], W=[b_cst])
        k.dma("sp", posi[:, :], w["apos"][:, :], W=[b_cst])
        k.op("dve", lambda e: e.memset(ones_f[:, :], 1.0), W=[b_cst])
        k.op("pool", lambda e: e.memset(vaug0[:, :, :], 1.0), W=[b_vaug])
        k.op("pool", lambda e: e.memset(vaug1[:, :, :], 0.0), W=[b_vaug])
        k.op("pool", lambda e: e.memset(vaug1[:, :, 0:1], 1.0), W=[b_vaug])
        rr_f = qf[:, 0:ntile * 16]
        rr_i = qif[:, 0:ntile * 16].bitcast(I32)
        k.op("dve", lambda e: e.tensor_copy(out=posf[:, :], in_=posi[:, :]), R=[b_cst], W=[b_tab])
        for (ct, st_, inv, half) in ((cosq, sinq, invq, 8), (cosi, sini, invi, 16)):
            pb = posf[:, :].unsqueeze(2).to_broadcast([128, ntile, half])
            ib = inv[:, :].unsqueeze(1).to_broadcast([128, ntile, half])
            k.op("dve", lambda e, st_=st_, pb=pb, ib=ib: e.tensor_tensor(out=st_[:, :, :], in0=pb, in1=ib, op=ALU.mult),
                 R=[b_tab, b_cst], W=[b_tab])
            k.op("dve", lambda e, ct=ct, st_=st_: e.tensor_scalar(out=ct[:, :, :], in0=st_[:, :, :], scalar1=PI / 2, scalar2=None,
                                                          op0=ALU.add), R=[b_tab], W=[b_tab])
            for tt in (st_, ct):
                n_ = ntile * half
                tf = tt[:, :, :].rearrange("p a b -> p (a b)")
                kf_ = rr_f[:, 0:n_]
                ki_ = rr_i[:, 0:n_]
                k.op("dve", lambda e, tf=tf, kf_=kf_: e.tensor_scalar(out=kf_, in0=tf, scalar1=1.0 / (2.0 * PI), scalar2=None, op0=ALU.mult),
                     R=[b_tab], W=[b_rr])
                k.op("dve", lambda e, kf_=kf_, ki_=ki_: e.tensor_copy(out=ki_, in_=kf_), R=[b_rr], W=[b_rr])
                k.op("dve", lambda e, kf_=kf_, ki_=ki_: e.tensor_copy(out=kf_, in_=ki_), R=[b_rr], W=[b_rr])
                k.op("dve", lambda e, tf=tf, kf_=kf_: e.scalar_tensor_tensor(out=tf, in0=kf_, scalar=-6.28125, in1=tf, op0=ALU.mult, op1=ALU.add),
                     R=[b_rr, b_tab], W=[b_tab])
                k.op("dve", lambda e, tf=tf, kf_=kf_: e.scalar_tensor_tensor(out=tf, in0=kf_, scalar=-(2.0 * PI - 6.28125), in1=tf,
                                                                           op0=ALU.mult, op1=ALU.add), R=[b_rr, b_tab], W=[b_tab])
                k.op("dve", lambda e, tf=tf, kf_=kf_: e.tensor_scalar(out=kf_, in0=tf, scalar1=PI, scalar2=-2.0 * PI, op0=ALU.is_gt, op1=ALU.mult),
                     R=[b_tab], W=[b_rr])
                k.op("dve", lambda e, tf=tf, kf_=kf_: e.tensor_tensor(out=tf, in0=tf, in1=kf_, op=ALU.add), R=[b_rr, b_tab], W=[b_tab])
                k.op("dve", lambda e, tf=tf, kf_=kf_: e.tensor_scalar(out=kf_, in0=tf, scalar1=-PI, scalar2=2.0 * PI, op0=ALU.is_lt, op1=ALU.mult),
                     R=[b_tab], W=[b_rr])
                k.op("dve", lambda e, tf=tf, kf_=kf_: e.tensor_tensor(out=tf, in0=tf, in1=kf_, op=ALU.add), R=[b_rr, b_tab], W=[b_tab])
                k.op("dve", lambda e, tf=tf: e.tensor_scalar(out=tf, in0=tf, scalar1=PI, scalar2=-PI, op0=ALU.min, op1=ALU.max),
                     R=[b_tab], W=[b_tab])
                k.op("act", lambda e, tf=tf: e.activation(out=tf, in_=tf, func=AF.Sin), R=[b_tab], W=[b_tab])

        k.barrier()

        def load_tile(ti):
            r = ti % NR
            k.dma("sp", hin[r][:, :], src[ti * 128:(ti + 1) * 128, :], R=[src_bufs[ti]], W=[b_hin[r]])

        for ti in range(min(NR, ntile)):
            load_tile(ti)
        WSC = (IDX_H ** -0.5) * (IDX_D ** -0.5)
        gi_box = [0]

        def gen_P1(i):
            r = i % NR
            W_ = (i + 1) * 128
            xt, b_xt = xT[i % 2], b_xT[i % 2]
            qT_, b_qT_ = qT[i % 3], b_qT[i % 3]
            score, b_score = score2[i % 2], b_score2[i % 2]
            to_feature_major(k, c, hin[r][:, :], b_hin[r], xt, b_xt, 0, cast_eng="act")
            tp, tp_b = c.tp[0], c.b_tp[0]
            for kc in range(8):
                k.op("pe", lambda e, kc=kc: e.matmul(G[0][:, :], xt[:, kc, :], w_in[:, kc, 0:512], start=(kc == 0), stop=(kc == 7)),
                     R=[b_xt, b_w], W=[b_G[0]])
            for kc in range(8):
                k.op("pe", lambda e, kc=kc: e.matmul(G[1][:, 0:136], xt[:, kc, :], w_in[:, kc, 512:648], start=(kc == 0), stop=(kc == 7)),
                     R=[b_xt, b_w], W=[b_G[1]])
            yield 3.0
            k.op("act", lambda e: e.activation(out=tcs[:, 0:256], in_=G[0][:, 0:256], func=AF.Square, accum_out=sm[:, 0:1]),
                 R=[b_G[0]], W=[b_tc, b_sm])
            k.op("dve", lambda e: e.tensor_scalar(out=sm[:, 1:2], in0=sm[:, 0:1], scalar1=1.0 / 256, scalar2=RMS_EPS,
                                                  op0=ALU.mult, op1=ALU.add), R=[b_sm], W=[b_sm])
            k.op("pool", lambda e: e.tensor_tensor(out=sm[:, 2:3], in0=sm[:, 1:2], in1=c.mhalf[:, 0:1], op=ALU.pow),
                 R=[b_sm, c.b_mh], W=[b_sm])
            k.op("dve", lambda e: e.scalar_tensor_tensor(out=cqn[:, :], in0=G[0][:, 0:256], scalar=sm[:, 2:3], in1=gcq[:, :],
                                                         op0=ALU.mult, op1=ALU.mult), R=[b_G[0], b_sm, b_cst], W=[b_cqn])
            for ch in range(2):
                k.op("pe", lambda e, ch=ch: e.transpose(out=tp[:, ch * 128:(ch + 1) * 128], in_=cqn[:, ch * 128:(ch + 1) * 128],
                                                        identity=c.ident[:, :]), R=[b_cqn, c.b_const], W=[tp_b])
            k.op("act", lambda e: e.copy(out=cqT[:, :, :], in_=tp[:, 0:256].rearrange("p (c t) -> p c t", c=2)), R=[tp_b], W=[b_cqT])
            yield 2.0
            k.op("act", lambda e: e.copy(out=kf[:, :], in_=G[0][:, 256:384]), R=[b_G[0]], W=[b_kf])
            k.op("act", lambda e: e.copy(out=vaug0[:, i, 0:64], in_=G[0][:, 384:448]), R=[b_G[0]], W=[b_vaug])
            k.op("act", lambda e: e.copy(out=vaug1[:, i, 64:128], in_=G[0][:, 448:512]), R=[b_G[0]], W=[b_vaug])
            rope_inplace(k, kf[:, :], b_kf, 2, 64, 8, cosq[:, i, :], sinq[:, i, :], b_tab, tcs, tss, b_tc, b_ts)
            k.op("act", lambda e: e.copy(out=kb_[:, 0:128], in_=kf[:, :]), R=[b_kf], W=[b_kb])
            yield 2.0
            k.op("dve", lambda e: e.bn_stats(out=lst[:, :], in_=G[1][:, 0:128]), R=[b_G[1]], W=[b_lst])
            k.op("dve", lambda e: e.bn_aggr(out=lmv[:, :], in_=lst[:, :]), R=[b_lst], W=[b_lmv])
            k.op("dve", lambda e: e.tensor_scalar(out=lmv[:, 1:2], in0=lmv[:, 1:2], scalar1=LN_EPS, scalar2=None, op0=ALU.add),
                 R=[b_lmv], W=[b_lmv])
            k.op("pool", lambda e: e.tensor_tensor(out=lrs[:, 0:1], in0=lmv[:, 1:2], in1=c.mhalf[:, 0:1], op=ALU.pow),
                 R=[b_lmv, c.b_mh], W=[b_lrs])
            k.op("dve", lambda e: e.scalar_tensor_tensor(out=lrs[:, 1:2], in0=lmv[:, 0:1], scalar=-1.0, in1=lrs[:, 0:1],
                                                         op0=ALU.mult, op1=ALU.mult), R=[b_lmv, b_lrs], W=[b_lnm])
            k.op("act", lambda e: e.activation(out=kif[:, :], in_=G[1][:, 0:128], func=AF.Identity, bias=lrs[:, 1:2],
                                               scale=lrs[:, 0:1]), R=[b_G[1], b_lrs, b_lnm], W=[b_kif])
            k.op("dve", lambda e: e.tensor_tensor(out=kif[:, :], in0=kif[:, :], in1=gik[:, :], op=ALU.mult), R=[b_kif, b_cst], W=[b_kif])
            k.op("dve", lambda e: e.tensor_tensor(out=kif[:, :], in0=kif[:, :], in1=bik[:, :], op=ALU.add), R=[b_kif, b_cst], W=[b_kif])
            rope_inplace(k, kif[:, :], b_kif, 1, 128, 16, cosi[:, i, :], sini[:, i, :], b_tab, tcs, tss, b_tc, b_ts)
            k.op("act", lambda e: e.copy(out=kb_[:, 128:256], in_=kif[:, :]), R=[b_kif], W=[b_kb])
            k.op("act", lambda e: e.activation(out=wabs[:, :], in_=G[1][:, 128:136], func=AF.Abs, scale=WSC),
                 R=[b_G[1]], W=[b_wabs])
            k.op("act", lambda e: e.activation(out=sgn[:, :], in_=G[1][:, 128:136], func=AF.Sign), R=[b_G[1]], W=[b_sgn])
            k.op("dve", lambda e: e.tensor_tensor(out=dsg[:, :, :], in0=c.ident[:, :].unsqueeze(1).to_broadcast([128, 8, 128]),
                                                  in1=sgn[:, :].unsqueeze(2).to_broadcast([128, 8, 128]), op=ALU.mult),
                 R=[c.b_const, b_sgn], W=[b_dsg])
            for ch in range(2):
                k.op("pe", lambda e, ch=ch: e.transpose(out=tp[:, (2 + ch) * 128:(3 + ch) * 128], in_=kb_[:, ch * 128:(ch + 1) * 128],
                                                        identity=c.ident[:, :]), R=[b_kb, c.b_const], W=[tp_b])
            k.op("dve", lambda e: e.tensor_copy(out=kT[:, i * 128:(i + 1) * 128], in_=tp[:, 256:384]), R=[tp_b], W=[b_kT])
            k.op("dve", lambda e: e.tensor_copy(out=kidxT[:, i * 128:(i + 1) * 128], in_=tp[:, 384:512]), R=[tp_b], W=[b_kidxT])
            yield 4.0
            for (wq, dstf, bdst) in ((w_uq, qf, b_qf), (w_iq, qif, b_qif)):
                for half in range(2):
                    for kc in range(2):
                        k.op("pe", lambda e, kc=kc, half=half, wq=wq: e.matmul(G[half][:, :], cqT[:, kc, :],
                                                                               wq[:, kc, half * 512:(half + 1) * 512],
                                                                               start=(kc == 0), stop=(kc == 1)),
                             R=[b_cqT, b_w], W=[b_G[half]])
                    k.op("act", lambda e, half=half, dstf=dstf: e.copy(out=dstf[:, half * 512:(half + 1) * 512], in_=G[half][:, :]),
                         R=[b_G[half]], W=[bdst])
            yield 3.0
            rope_inplace(k, qf[:, :], b_qf, 16, 64, 8, cosq[:, i, :], sinq[:, i, :], b_tab, tcs, tss, b_tc, b_ts)
            k.op("dve", lambda e: e.tensor_copy(out=qb[:, :], in_=qf[:, :]), R=[b_qf], W=[b_qb])
            rope_inplace(k, qif[:, :], b_qif, 8, 128, 16, cosi[:, i, :], sini[:, i, :], b_tab, tcs, tss, b_tc, b_ts)
            k.op("dve", lambda e: e.tensor_tensor(out=qib[:, :].rearrange("p (h d) -> p h d", h=8),
                                                  in0=qif[:, :].rearrange("p (h d) -> p h d", h=8),
                                                  in1=wabs[:, :].unsqueeze(2).to_broadcast([128, 8, 128]), op=ALU.mult),
                 R=[b_qif, b_wabs], W=[b_qib])
            yield 4.0
            for (srcb, bsrc, dT, bd) in ((qib, b_qib, qidxT, b_qidxT), (qb, b_qb, qT_, b_qT_)):
                for h in range(8):
                    k.op("pe", lambda e, h=h, srcb=srcb: e.transpose(out=tp[:, h * 128:(h + 1) * 128],
                                                                      in_=srcb[:, h * 128:(h + 1) * 128], identity=c.ident[:, :]),
                         R=[bsrc, c.b_const], W=[tp_b])
                k.op("act", lambda e, dT=dT: e.copy(out=dT[:, :, :], in_=tp[:, :].rearrange("p (h t) -> p h t", h=8)),
                     R=[tp_b], W=[bd])
                yield 1.5
            nkb = (W_ + 511) // 512
            stepsl = [(kb, h) for kb in range(nkb) for h in range(8)]
            base = gi_box[0]
            gi_box[0] += len(stepsl)

            def iq(sidx):
                kb, h = stepsl[sidx]
                wd = min(512, W_ - kb * 512)
                g_, bg_ = G[(base + sidx) % 2], b_G[(base + sidx) % 2]
                k.op("pe", lambda e: e.matmul(g_[:, 0:wd], qidxT[:, h, :], kidxT[:, kb * 512:kb * 512 + wd],
                                              start=True, stop=True), R=[b_qidxT, b_kidxT], W=[bg_])
            iq(0)
            for sidx, (kb, h) in enumerate(stepsl):
                wd = min(512, W_ - kb * 512)
                g_, bg_ = G[(base + sidx) % 2], b_G[(base + sidx) % 2]
                rs_, brs_ = rsb[(base + sidx) % 2], b_rsb[(base + sidx) % 2]
                if h % 4 != 3:
                    k.op("act", lambda e: e.activation(out=rs_[:, 0:wd], in_=g_[:, 0:wd], func=AF.Relu), R=[bg_], W=[brs_])
                else:
                    k.op("dve", lambda e: e.tensor_scalar(out=rs_[:, 0:wd], in0=g_[:, 0:wd], scalar1=0.0, scalar2=None, op0=ALU.max),
                         R=[bg_], W=[brs_])
                if sidx + 1 < len(stepsl):
                    iq(sidx + 1)
                if FILL_IDX:
                    filler(FILL_N)
                k.op("pe", lambda e: e.matmul(sc[:, 0:wd], dsg[:, h, :], rs_[:, 0:wd], start=(h == 0), stop=(h == 7)),
                     R=[b_dsg, brs_], W=[b_sc])
                if h % 2 == 1:
                    yield 1.0 * wd / 512
                if h == 7:
                    k.op("dve", lambda e: e.tensor_copy(out=score[:, kb * 512:kb * 512 + wd], in_=sc[:, 0:wd]), R=[b_sc], W=[b_score])
                    yield 0.6
            search = i * 128 >= topk
            if search:
                k.op("dve", lambda e: e.tensor_reduce(out=amax[i % 2][:, 0:1], in_=score[:, 0:W_], axis=AX.X, op=ALU.max,
                                                      apply_absolute_value=True), R=[b_score], W=[b_amax[i % 2]])
            k.op("dve", lambda e: e.tensor_tensor(out=score[:, i * 128:W_], in0=score[:, i * 128:W_], in1=dmask[:, :], op=ALU.add),
                 R=[b_score, b_cst], W=[b_score])
            yield 3.0 * W_ / 2048

        def gen_P2(i):
            W_ = (i + 1) * 128
            mT, b_mT = maskT[i % 2], b_maskT[i % 2]
            score, b_score = score2[i % 2], b_score2[i % 2]
            tp, tp_b = ot[1][:, :].bitcast(BF16), c.b_fillps
            search = i * 128 >= topk
            if search:
                k.op("dve", lambda e: e.tensor_scalar(out=bis[:, 1:2], in0=amax[i % 2][:, 0:1], scalar1=2.002, scalar2=2e-6,
                                                      op0=ALU.mult, op1=ALU.add), R=[b_amax[i % 2]], W=[b_bis])
                k.op("dve", lambda e: e.tensor_scalar(out=steps[:, :], in0=pow2[:, :], scalar1=bis[:, 1:2], scalar2=None,
                                                      op0=ALU.mult), R=[b_bis, b_cst], W=[b_steps])
                k.op("dve", lambda e: e.memset(bis[:, 2:3], 0.0), W=[b_mid])
                for n in range(NIT):
                    k.op("dve", lambda e: e.tensor_scalar(out=junk[:, 0:W_], in0=score[:, 0:W_], scalar1=bis[:, 2:3], scalar2=0.0,
                                                          op0=ALU.is_ge, op1=ALU.add, accum_out=bis[:, 3:4]),
                         R=[b_score, b_mid], W=[b_maskb, b_cnt])
                    k.op("dve", lambda e, n=n: e.tensor_scalar(out=bis[:, 4:5], in0=bis[:, 3:4], scalar1=float(topk) - 0.5,
                                                               scalar2=steps[:, n:n + 1], op0=ALU.is_ge, op1=ALU.mult),
                         R=[b_cnt, b_steps], W=[b_t])
                    k.op("dve", lambda e, n=n: e.scalar_tensor_tensor(out=bis[:, 2:3], in0=bis[:, 4:5], scalar=steps[:, n + 1:n + 2],
                                                                      in1=bis[:, 2:3], op0=ALU.subtract, op1=ALU.add),
                         R=[b_t, b_steps, b_mid], W=[b_mid])
                    yield 0.6 + 3.0 * W_ / 2048
                k.op("dve", lambda e: e.tensor_tensor(out=bis[:, 5:6], in0=bis[:, 2:3], in1=steps[:, NIT:NIT + 1], op=ALU.subtract),
                     R=[b_mid, b_steps], W=[b_bis])
                k.op("dve", lambda e: e.tensor_scalar(out=maskb[:, 0:W_], in0=score[:, 0:W_], scalar1=bis[:, 5:6], scalar2=MASK_NEG,
                                                      op0=ALU.is_lt, op1=ALU.mult), R=[b_score, b_bis], W=[b_maskb])
            else:
                k.op("dve", lambda e: e.tensor_scalar(out=maskb[:, 0:W_], in0=score[:, 0:W_], scalar1=-1e29, scalar2=MASK_NEG,
                                                      op0=ALU.is_lt, op1=ALU.mult), R=[b_score], W=[b_maskb])
            yield 2.0 * W_ / 2048
            for b0 in range(0, i + 1, 8):
                nb = min(8, i + 1 - b0)
                for j in range(nb):
                    k.op("pe", lambda e, j=j, b0=b0: e.transpose(out=tp[:, j * 128:(j + 1) * 128],
                                                                 in_=maskb[:, (b0 + j) * 128:(b0 + j + 1) * 128], identity=c.ident[:, :]),
                         R=[b_maskb, c.b_const], W=[tp_b])
                k.op("act", lambda e, b0=b0, nb=nb: e.copy(out=mT[:, b0:b0 + nb, :],
                                                           in_=tp[:, 0:nb * 128].rearrange("p (b t) -> p b t", b=nb)),
                     R=[tp_b], W=[b_mT])
                yield 1.5

        def gen_Q(i):
            r = i % NR
            qT_, b_qT_ = qT[i % 3], b_qT[i % 3]
            mT, b_mT = maskT[i % 2], b_maskT[i % 2]
            mm = 0
            for quad in range(4):
                g, half = quad // 2, quad % 2
                o_, bo_ = ot[0], b_ot[0]
                rows = slice(g * 64, (g + 1) * 64)
                qrhs = qT_[rows, half * 4:(half + 1) * 4, :]

                def qk(kb):
                    s_, bs_ = st3[kb % 3], b_st3[kb % 3]
                    k.op("pe", lambda e: e.matmul(s_[:, :], kT[rows, kb * 128:(kb + 1) * 128], qrhs, start=True, stop=False),
                         R=[b_kT, b_qT_], W=[bs_])
                    k.op("pe", lambda e: e.matmul(s_[:, :], c.ident[:, :], mT[:, kb, :].unsqueeze(1).to_broadcast([128, 4, 128]),
                                                  start=False, stop=True), R=[c.b_const, b_mT], W=[bs_])
                qk(0)
                if i >= 1:
                    qk(1)
                for kb in range(i + 1):
                    if kb + 2 <= i:
                        qk(kb + 2)
                    s_, bs_ = st3[kb % 3], b_st3[kb % 3]
                    e_, be_ = esb[kb % 4], b_esb[kb % 4]
                    k.op("act", lambda e, s_=s_, e_=e_: e.activation(out=e_[:, :, :], in_=s_[:, :].rearrange("p (h t) -> p h t", h=4),
                                                                      func=AF.Exp, scale=0.125), R=[bs_], W=[be_])
                    if FILL_ATT:
                        filler(FILL_N)
                        if FILL_ATT > 1:
                            filler(FILL_N)
                    if g == 0:
                        k.op("pe", lambda e, e_=e_, kb=kb: e.matmul(o_[0:65, :], vaug0[:, kb, :], e_[:, :, :].rearrange("p h t -> p (h t)"),
                                                                    start=(kb == 0), stop=(kb == i)), R=[b_vaug, be_], W=[bo_])
                    else:
                        k.op("pe", lambda e, e_=e_, kb=kb: e.matmul(o_[:, :], vaug1[:, kb, :], e_[:, :, :].rearrange("p h t -> p (h t)"),
                                                                    start=(kb == 0), stop=(kb == i)), R=[b_vaug, be_], W=[bo_])
                    yield 1.0
                if g == 0:
                    k.op("act", lambda e, o_=o_, quad=quad: e.copy(out=oT[0:65, quad, :], in_=o_[0:65, :]), R=[bo_], W=[b_oT])
                else:
                    k.op("act", lambda e, o_=o_, quad=quad: e.copy(out=oT[:, quad, :], in_=o_[:, :]), R=[bo_], W=[b_oT])
                yield 0.7
            for quad in range(4):
                dr = 64 if quad < 2 else 0
                for hh in range(4):
                    k.op("pe", lambda e, hh=hh, quad=quad, dr=dr: e.matmul(G_q[:, quad * 4 + hh:quad * 4 + hh + 1],
                                                                           oT[dr:dr + 1, quad, hh * 128:(hh + 1) * 128],
                                                                           ones_f[dr:dr + 1, 0:1], start=True, stop=True),
                         R=[b_oT, b_cst], W=[b_Gq])
            yield 1.0
            k.op("dve", lambda e: e.reciprocal(out=rden[:, :], in_=G_q[:, 0:16]), R=[b_Gq], W=[b_rden])
            for quad in range(4):
                g, half = quad // 2, quad % 2
                s_, bs_ = st[quad % 2], b_st[quad % 2]
                for hh in range(4):
                    idx = quad * 4 + hh
                    k.op("pe", lambda e, hh=hh, idx=idx, s_=s_: e.matmul(s_[:, hh * 128:(hh + 1) * 128],
                                                                          rden[:, idx:idx + 1].to_broadcast([128, 128]), identf[:, :],
                                                                          start=True, stop=True), R=[b_rden, b_cst], W=[bs_])
                rows = slice(g * 64, (g + 1) * 64)
                k.op("dve", lambda e, quad=quad, s_=s_, rows=rows, half=half: e.tensor_tensor(
                    out=onrm[rows, half * 4:(half + 1) * 4, :].rearrange("p h t -> p (h t)"), in0=oT[rows, quad, :],
                    in1=s_[rows, :], op=ALU.mult), R=[b_oT, bs_], W=[b_onrm])
                yield 1.5
            for half in range(2):
                o_, bo_ = (ot[0], b_ot[0]) if half == 0 else (st[1], b_st[1])
                for j in range(8):
                    k.op("pe", lambda e, j=j, half=half, o_=o_: e.matmul(o_[:, :], onrm[:, j, :], w_o[:, j, half * 512:(half + 1) * 512],
                                                                         start=(j == 0), stop=(j == 7)), R=[b_onrm, b_w], W=[bo_])
                k.op("dve", lambda e, half=half, o_=o_: e.scalar_tensor_tensor(
                    out=hin[r][:, half * 512:(half + 1) * 512], in0=hin[r][:, half * 512:(half + 1) * 512],
                    scalar=ALPHA, in1=o_[:, :], op0=ALU.mult, op1=ALU.add), R=[b_hin[r], bo_], W=[b_hin[r]])
                yield 2.0
            layer_norm_inplace(k, c, hin[r][:, :], b_hin[r], gbc[:, :], bbc[:, :], b_gb, tag)
            k.dma("sp", dst[i * 128:(i + 1) * 128, :], hin[r][:, :], R=[b_hin[r]], W=[dst_bufs[i]])
            if i + NR < ntile:
                load_tile(i + NR)
            yield 5.0

        run_streams([gen_P1(0)])
        g0 = [gen_P2(0)]
        if ntile > 1:
            g0.append(gen_P1(1))
        run_streams(g0)
        for i in range(ntile):
            gens = [gen_Q(i)]
            if i + 1 < ntile:
                gens.append(gen_P2(i + 1))
            if i + 2 < ntile:
                gens.append(gen_P1(i + 2))
            run_streams(gens)


def build_program(T=4096, passes=("ffn0",), dbg=False):
    nc = bass.Bass("TRN2", target_bir_lowering=False)
    ntile = T // 128
    dt = {}

    def din(name, shape, dtype=F32):
        dt[name] = nc.dram_tensor(name, list(shape), dtype, kind="ExternalInput").ap()
        return dt[name]

    x = din("x", [T, D])
    ident_d = din("ident", [128, 128])
    ffn_w = []
    for l in range(2):
        ffn_w.append((din("wg%d" % l, [128, 8 * DFF]), din("wu%d" % l, [128, 8 * DFF]),
                      din("wd%d" % l, [128, NF * D])))
    hw = {"hw_in": din("hw_in", [128, 8 * 4096]), "hw_o": din("hw_o", [128, 8 * D]),
          "hU": din("hU", [4, 128, 128]), "hmtri": din("hmtri", [128, 128]), "hones": din("hones", [128, 1]),
          "hgn": din("hgn", [1, D]), "hlbl": din("hlbl", [2, D])}
    aw = {"aw_in": din("aw_in", [128, 8 * 648]), "aw_uq": din("aw_uq", [128, 2 * D]), "aw_iq": din("aw_iq", [128, 2 * D]),
          "aw_o": din("aw_o", [128, 8 * D]), "aidentf": din("aidentf", [128, 128]), "agcq": din("agcq", [1, 256]), "agik": din("agik", [1, 128]),
          "abik": din("abik", [1, 128]), "ainvq": din("ainvq", [1, 8]), "ainvi": din("ainvi", [1, 16]),
          "apow2": din("apow2", [1, NIT + 1]), "admask": din("admask", [128, 128]), "apos": din("apos", [128, ntile], I32)}
    ln_g = din("ln_g", [4, D])
    ln_b = din("ln_b", [4, D])
    out = nc.dram_tensor("out", [T, D], F32, kind="ExternalOutput").ap()
    scr = [nc.dram_tensor("scr%d" % i, [T, D], F32, kind="Internal").ap() for i in range(3)]

    with ExitStack() as es:
        k = K(nc, es)
        c = Ctx()
        c.ident = sb(nc, es, "ident_sb", [128, 128], BF16)
        c.b_const = k.buf("const")
        k.dma("pool", c.ident[:, :], ident_d[:, :], W=[c.b_const])
        c.b_fill = k.buf("fill")
        c.b_fillps = k.buf("fillps")
        c.mhalf = sb(nc, es, "mhalf", [128, 16], F32)
        c.b_mh = k.buf("mhalf")
        k.op("dve", lambda e: e.memset(c.mhalf[:, :], -0.5), W=[c.b_mh])
        c.ln_stats = sb(nc, es, "ln_stats", [128, 2, 6], F32)
        c.ln_mv = sb(nc, es, "ln_mv", [128, 2], F32)
        c.ln_rs = sb(nc, es, "ln_rs", [128, 2], F32)
        c.b_lnst, c.b_lnmv, c.b_lnrs, c.b_lnnm = k.bufs(4, "ln")
        c.hb = [sb(nc, es, "hb%d" % i, [128, D], BF16) for i in range(1)]
        c.b_hb = k.bufs(1, "hb")
        c.tp_i = 0

        x_bufs = k.bufs(ntile, "x")
        scr_bufs = [k.bufs(ntile, "scr%d_" % i) for i in range(3)]
        out_bufs = k.bufs(ntile, "out")

        cur, cur_b = x, x_bufs
        plan = list(passes)
        for pi, p in enumerate(plan):
            last = pi == len(plan) - 1
            dstt, dst_b = (out, out_bufs) if last else (scr[pi % 3], scr_bufs[pi % 3])
            if p in ("ffn0", "ffn1"):
                l = int(p[-1])
                ffn_pass(k, c, T, cur, cur_b, dstt, dst_b, ffn_w[l][0], ffn_w[l][1], ffn_w[l][2],
                         ln_g[2 * l + 1:2 * l + 2, :], ln_b[2 * l + 1:2 * l + 2, :], es, p)
            elif p == "dsa":
                dsa_pass(k, c, T, cur, cur_b, dstt, dst_b, aw, ln_g[0:1, :], ln_b[0:1, :], p)
            elif p == "hgrn":
                hgrn_pass(k, c, T, cur, cur_b, dstt, dst_b, hw, ln_g[2:3, :], ln_b[2:3, :], p)
            cur, cur_b = dstt, dst_b
            if not last:
                k.barrier()
        k.finish(out_bufs)
    return nc


def wlayout(w, nchunk):
    n = w.shape[1]
    return np.ascontiguousarray(w.reshape(nchunk, 128, n).transpose(1, 0, 2).reshape(128, nchunk * n))


def make_in_maps(inputs, T=4096, ncores=8):
    f = np.float32
    shared = {"ident": np.eye(128, dtype=f)}
    for l in range(2):
        shared["wg%d" % l] = wlayout(np.asarray(inputs["ffn_w_gate"][l], f), 8)
        shared["wu%d" % l] = wlayout(np.asarray(inputs["ffn_w_up"][l], f), 8)
        shared["wd%d" % l] = wlayout(np.asarray(inputs["ffn_w_down"][l], f), NF)
    shared["aw_in"] = wlayout(np.asarray(inputs["att_w_in"][0], f), 8)
    perm = np.concatenate([np.r_[j * 64:(j + 1) * 64, (8 + j) * 64:(9 + j) * 64] for j in range(8)])
    shared["aw_uq"] = wlayout(np.asarray(inputs["att_w_uq"][0], f)[:, perm], 2)
    shared["aw_iq"] = wlayout(np.asarray(inputs["att_w_iq"][0], f), 2)
    shared["aw_o"] = np.ascontiguousarray(np.asarray(inputs["att_w_o"][0], f).reshape(2, 8, 64, D).transpose(0, 2, 1, 3).reshape(128, 8 * D))
    shared["aidentf"] = np.eye(128, dtype=f)
    shared["agcq"] = np.ascontiguousarray(np.asarray(inputs["att_g_cq"], f).reshape(1, 256))
    shared["agik"] = np.ascontiguousarray(np.asarray(inputs["att_g_ik"], f).reshape(1, 128))
    shared["abik"] = np.ascontiguousarray(np.asarray(inputs["att_b_ik"], f).reshape(1, 128))
    shared["ainvq"] = (np.float32(ROPE_THETA) ** (-np.arange(0, 16, 2, dtype=f) / np.float32(16))).astype(f)[None, :]
    shared["ainvi"] = (np.float32(ROPE_THETA) ** (-np.arange(0, 32, 2, dtype=f) / np.float32(32))).astype(f)[None, :]
    shared["apow2"] = (2.0 ** -(np.arange(NIT + 1, dtype=f) + 1)).astype(f)[None, :]
    qi_ = np.arange(128)[:, None]
    si_ = np.arange(128)[None, :]
    shared["admask"] = np.where(si_ <= qi_, 0.0, -1e30).astype(f)
    shared["hw_in"] = wlayout(np.asarray(inputs["hgrn_w_in"][0], f), 8)
    shared["hw_o"] = wlayout(np.asarray(inputs["hgrn_w_o"][0], f), 8)
    si = np.arange(128)[:, None]
    ti = np.arange(128)[None, :]
    same = (si // 64) == (ti // 64)
    U1 = (same & (si <= ti)).astype(f)
    U2 = (si > ti).astype(f)
    U3 = np.where(ti < 64, ((si < 64) & (si > ti)).astype(f), -((si >= 64) & (si <= ti)).astype(f)).astype(f)
    U4 = (si <= ti).astype(f)
    shared["hU"] = np.ascontiguousarray(np.stack([U1, U2, U3, U4]))
    shared["hmtri"] = (si <= ti).astype(f)
    shared["hones"] = np.ones((128, 1), f)
    shared["hgn"] = np.ascontiguousarray(np.tile(np.asarray(inputs["hgrn_g_norm"][0], f), 8)[None, :])
    shared["hlbl"] = np.ascontiguousarray(np.asarray(inputs["hgrn_lb_logits"], f))
    shared["ln_g"] = np.ascontiguousarray(np.asarray(inputs["ln_g"], f).reshape(4, D))
    shared["ln_b"] = np.ascontiguousarray(np.asarray(inputs["ln_b"], f).reshape(4, D))
    maps = []
    for b in range(ncores):
        m = dict(shared)
        m["x"] = np.ascontiguousarray(np.asarray(inputs["x"][b, :T], f))
        m["apos"] = np.ascontiguousarray(np.asarray(inputs["positions"][b, :T], np.int32).reshape(T // 128, 128).T)
        maps.append(m)
    return maps


def kernel(**inputs):
    T = 4096
    nc = build_program(T, passes=("dsa", "ffn0", "hgrn", "ffn1"))
    maps = make_in_maps(inputs, T, 8)
    res = run_bass_kernel_spmd(nc, maps, core_ids=list(range(8)))
    return np.stack([np.asarray(r["out"], np.float32) for r in res.results], axis=0)
```

```python
from contextlib import ExitStack
import numpy as np
import concourse.bass as bass
import concourse.mybir as mybir
from concourse.bass_utils import run_bass_kernel_spmd

F32 = mybir.dt.float32
BF16 = mybir.dt.bfloat16
I32 = mybir.dt.int32
AF = mybir.ActivationFunctionType
ALU = mybir.AluOpType
AX = mybir.AxisListType

D = 1024
DFF = 2816
NF = DFF // 128
DEPTH = 2
ALPHA = (2 * DEPTH) ** 0.25
LN_EPS = 1e-5
RMS_EPS = 1e-6
IDX_H = 8
IDX_D = 128
ROPE_THETA = 500000.0


class Buf:
    __slots__ = ("name", "w", "r")

    def __init__(self, name):
        self.name = name
        self.w = None
        self.r = []


class Eng:
    def __init__(self, name, e, sem):
        self.name = name
        self.e = e
        self.sem = sem
        self.cnt = 0
        self.seen = {}
        self.ring = []
        self.ring_i = 0


class K:
    def __init__(self, nc, es, ring=8):
        self.nc = nc
        self.es = es
        self.sems = {}
        self.eng = {}
        for name, e in (("pe", nc.tensor), ("dve", nc.vector), ("act", nc.scalar),
                        ("pool", nc.gpsimd), ("sp", nc.sync)):
            sem = es.enter_context(nc.semaphore("s_" + name))
            self.sems[name] = sem
            self.eng[name] = Eng(name, e, sem)
        for q in ("sp", "pool", "act"):
            for j in range(ring):
                key = "d_%s%d" % (q, j)
                self.sems[key] = es.enter_context(nc.semaphore(key))
                self.eng[q].ring.append([key, 0])
        self.nbuf = 0

    def buf(self, name=None):
        self.nbuf += 1
        return Buf(name or ("b%d" % self.nbuf))

    def bufs(self, n, name="b"):
        return [self.buf("%s%d" % (name, i)) for i in range(n)]

    def _deps(self, en, R, W):
        need = {}
        for b in R:
            if b.w is not None:
                k, v, src = b.w
                if src == "pe" and en == "pe":
                    continue
                need[k] = max(need.get(k, 0), v)
        for b in W:
            toks = list(b.r)
            if b.w is not None:
                toks.append(b.w)
            for k, v, src in toks:
                if src == en and en == "pe":
                    continue
                need[k] = max(need.get(k, 0), v)
        E = self.eng[en]
        for k, v in need.items():
            if E.seen.get(k, 0) < v:
                E.e.wait_ge(self.sems[k], v)
                E.seen[k] = v

    def _commit(self, tok, R, W):
        for b in R:
            b.r.append(tok)
        for b in W:
            b.w = tok
            b.r = []

    def op(self, en, fn, R=(), W=()):
        self._deps(en, R, W)
        E = self.eng[en]
        ins = fn(E.e)
        E.cnt += 1
        ins.then_inc(E.sem, 1)
        self._commit((en, E.cnt, en), R, W)

    def dma(self, q, out, in_, R=(), W=(), **kw):
        E = self.eng[q]
        self._deps(q, R, W)
        slot = E.ring[E.ring_i % len(E.ring)]
        E.ring_i += 1
        key = slot[0]
        if slot[1] > 0 and E.seen.get(key, 0) < slot[1]:
            E.e.wait_ge(self.sems[key], slot[1])
            E.seen[key] = slot[1]
        ins = E.e.dma_start(out=out, in_=in_, **kw)
        slot[1] += 16
        ins.then_inc(self.sems[key], 16)
        self._commit((key, slot[1], "dma_" + q), R, W)

    def barrier(self):
        cur = {}
        for name, E in self.eng.items():
            if E.cnt:
                cur[name] = E.cnt
            for key, v in E.ring:
                if v:
                    cur[key] = v
        for name, E in self.eng.items():
            for key, v in cur.items():
                if key == name and name == "pe":
                    continue
                if E.seen.get(key, 0) < v:
                    E.e.wait_ge(self.sems[key], v)
                    E.seen[key] = v

    def finish(self, bufs):
        E = self.eng["sp"]
        for b in bufs:
            if b.w is not None:
                k, v, _ = b.w
                if E.seen.get(k, 0) < v:
                    E.e.wait_ge(self.sems[k], v)
                    E.seen[k] = v


def sb(nc, es, name, shape, dt):
    return es.enter_context(nc.sbuf_tensor("sb_" + name, list(shape), dt))


class Ctx:
    pass


def layer_norm_inplace(k, c, y, yb, gbc, bbc, g_b, tag):
    nc = k.nc
    st = c.ln_stats
    k.op("dve", lambda e: e.bn_stats(out=st[:, 0, :], in_=y[:, 0:512]), R=[yb], W=[c.b_lnst])
    k.op("dve", lambda e: e.bn_stats(out=st[:, 1, :], in_=y[:, 512:1024]), R=[yb], W=[c.b_lnst])
    k.op("dve", lambda e: e.bn_aggr(out=c.ln_mv[:, :], in_=st[:, :, :].rearrange("p a b -> p (a b)")),
         R=[c.b_lnst], W=[c.b_lnmv])
    k.op("dve", lambda e: e.tensor_scalar(out=c.ln_mv[:, 1:2], in0=c.ln_mv[:, 1:2], scalar1=LN_EPS, scalar2=None, op0=ALU.add),
         R=[c.b_lnmv], W=[c.b_lnmv])
    k.op("pool", lambda e: e.tensor_tensor(out=c.ln_rs[:, 0:1], in0=c.ln_mv[:, 1:2], in1=c.mhalf[:, 0:1], op=ALU.pow),
         R=[c.b_lnmv, c.b_mh], W=[c.b_lnrs])
    k.op("dve", lambda e: e.scalar_tensor_tensor(out=c.ln_rs[:, 1:2], in0=c.ln_mv[:, 0:1], scalar=-1.0,
                                                 in1=c.ln_rs[:, 0:1], op0=ALU.mult, op1=ALU.mult),
         R=[c.b_lnmv, c.b_lnrs], W=[c.b_lnnm])
    k.op("act", lambda e: e.activation(out=y, in_=y, func=AF.Identity, bias=c.ln_rs[:, 1:2],
                                       scale=c.ln_rs[:, 0:1]),
         R=[yb, c.b_lnrs, c.b_lnnm], W=[yb])
    k.op("dve", lambda e: e.tensor_tensor(out=y, in0=y, in1=gbc, op=ALU.mult), R=[yb, g_b], W=[yb])
    k.op("pool", lambda e: e.tensor_tensor(out=y, in0=y, in1=bbc, op=ALU.add), R=[yb, g_b], W=[yb])


def load_bcast(k, q, dst, src_row, b):
    k.dma(q, dst, src_row.partition_broadcast(128), W=[b])


def to_feature_major(k, c, src, src_b, dstT, dst_b, col0, ncols=128, nchunks=8, cast_eng="pool"):
    i = c.tp_i
    c.tp_i += 1
    hb, hb_b = c.hb[i % len(c.hb)], c.b_hb[i % len(c.hb)]
    tp, tp_b = c.tp[i % len(c.tp)], c.b_tp[i % len(c.tp)]
    n = nchunks * 128
    if cast_eng == "act":
        k.op("act", lambda e: e.copy(out=hb[:, 0:n], in_=src), R=[src_b], W=[hb_b])
    else:
        k.op(cast_eng, lambda e: e.tensor_copy(out=hb[:, 0:n], in_=src), R=[src_b], W=[hb_b])
    for ch in range(nchunks):
        k.op("pe", lambda e, ch=ch: e.transpose(out=tp[:, ch * 128:(ch + 1) * 128],
                                                in_=hb[:, ch * 128:(ch + 1) * 128], identity=c.ident[:, :]),
             R=[hb_b, c.b_const], W=[tp_b])
    k.op("act", lambda e: e.copy(out=dstT[:, 0:nchunks, col0:col0 + 128],
                                 in_=tp[:, 0:n].rearrange("p (c t) -> p c t", c=nchunks)),
         R=[tp_b], W=[dst_b])


def run_streams(gens):
    acc = [0.0] * len(gens)
    alive = list(range(len(gens)))
    while alive:
        j = min(alive, key=lambda a_: acc[a_])
        try:
            acc[j] += next(gens[j])
        except StopIteration:
            alive.remove(j)


def ffn_pass(k, c, T, src, src_bufs, dst, dst_bufs, wg_d, wu_d, wd_d, lng_d, lnb_d, es_outer, tag):
    nc = k.nc
    with ExitStack() as es:
        wg = sb(nc, es, "wg" + tag, [128, 8, DFF], BF16)
        wu = sb(nc, es, "wu" + tag, [128, 8, DFF], BF16)
        wd = sb(nc, es, "wd" + tag, [128, NF, D], BF16)
        gbc = sb(nc, es, "gbc" + tag, [128, D], F32)
        bbc = sb(nc, es, "bbc" + tag, [128, D], F32)
        NR = 8
        hin = [sb(nc, es, "hin%s%d" % (tag, i), [128, D], F32) for i in range(NR)]
        hT = [sb(nc, es, "hT%s%d" % (tag, i), [128, 8, 512], BF16) for i in range(1)]
        actT = sb(nc, es, "actT" + tag, [128, NF, 512], BF16)
        sg = [sb(nc, es, "sg%s%d" % (tag, i), [128, 512], BF16) for i in range(2)]
        NFG = NF // 2
        b_wg = k.bufs(NFG, "wg" + tag)
        b_wu = k.bufs(NFG, "wu" + tag)
        b_wd = k.bufs(NFG, "wd" + tag)
        b_gb = k.buf()
        b_hin = k.bufs(NR, "hin" + tag)
        b_hT = k.bufs(1)
        b_act = k.bufs(NF, "act" + tag)
        b_sg = k.bufs(2)
        ntile = T // 128
        ngrp = T // 512

        def load_tile(ti):
            r = ti % NR
            k.dma("sp", hin[r][:, :], src[ti * 128:(ti + 1) * 128, :], R=[src_bufs[ti]], W=[b_hin[r]])

        for ti in range(min(8, ntile)):
            load_tile(ti)
        load_bcast(k, "sp", gbc[:, :], lng_d, b_gb)
        load_bcast(k, "sp", bbc[:, :], lnb_d, b_gb)
        wg3 = wg_d.rearrange("p (c f) -> p c f", c=8)
        wu3 = wu_d.rearrange("p (c f) -> p c f", c=8)
        for fg in range(NFG):
            cs = slice(fg * 256, (fg + 1) * 256)
            k.dma("pool", wg[:, :, cs], wg3[:, :, cs], W=[b_wg[fg]])
            k.dma("pool", wu[:, :, cs], wu3[:, :, cs], W=[b_wu[fg]])
        for fg in range(NFG):
            k.dma("pool", wd[:, 2 * fg:2 * fg + 2, :], wd_d[:, 2 * fg * D:(2 * fg + 2) * D].rearrange("p (f d) -> p f d", f=2),
                  W=[b_wd[fg]])
        ps = [es.enter_context(nc.psum_tensor("ps%s%d" % (tag, i), [128, 512], F32)) for i in range(6)]
        bps = k.bufs(6, "ps")
        c.tp = [es.enter_context(nc.psum_tensor("tp%s%d" % (tag, i), [128, 1024], BF16)) for i in range(2)]
        c.b_tp = k.bufs(2, "tp")

        def feat(g):
            for t in range(4):
                ti = g * 4 + t
                to_feature_major(k, c, hin[ti % NR][:, :], b_hin[ti % NR], hT[0], b_hT[0], t * 128, cast_eng="act")

        feat(0)
        for g in range(ngrp):
            hTg, b_hTg = hT[0], b_hT[0]
            for f in range(NF):
                pg, bg = ps[f % 2], bps[f % 2]
                pu, bu = ps[2 + f % 2], bps[2 + f % 2]
                for kc in range(8):
                    k.op("pe", lambda e, kc=kc: e.matmul(pg[:, :], wg[:, kc, f * 128:(f + 1) * 128], hTg[:, kc, :],
                                                         start=(kc == 0), stop=(kc == 7)),
                         R=[b_wg[f // 2], b_hTg], W=[bg])
                for kc in range(8):
                    k.op("pe", lambda e, kc=kc: e.matmul(pu[:, :], wu[:, kc, f * 128:(f + 1) * 128], hTg[:, kc, :],
                                                         start=(kc == 0), stop=(kc == 7)),
                         R=[b_wu[f // 2], b_hTg], W=[bu])
                s_, bs = sg[f % 2], b_sg[f % 2]
                k.op("act", lambda e: e.activation(out=s_[:, :], in_=pg[:, :], func=AF.Silu), R=[bg], W=[bs])
                k.op("dve", lambda e: e.tensor_tensor(out=actT[:, f, :], in0=s_[:, :], in1=pu[:, :], op=ALU.mult),
                     R=[bs, bu], W=[b_act[f]])
            if g + 1 < ngrp:
                feat(g + 1)
            for t in range(4):
                ti = g * 4 + t
                r = ti % NR
                for half in range(2):
                    py, by = ps[4 + half], bps[4 + half]
                    for f in range(NF):
                        k.op("pe", lambda e, f=f: e.matmul(py[:, :], actT[:, f, t * 128:(t + 1) * 128],
                                                           wd[:, f, half * 512:(half + 1) * 512],
                                                           start=(f == 0), stop=(f == NF - 1)),
                             R=[b_act[f], b_wd[f // 2]], W=[by])
                    k.op("dve", lambda e: e.scalar_tensor_tensor(
                        out=hin[r][:, half * 512:(half + 1) * 512], in0=hin[r][:, half * 512:(half + 1) * 512],
                        scalar=ALPHA, in1=py[:, :], op0=ALU.mult, op1=ALU.add), R=[b_hin[r], by], W=[b_hin[r]])
                layer_norm_inplace(k, c, hin[r][:, :], b_hin[r], gbc[:, :], bbc[:, :], b_gb, tag)
                k.dma("sp", dst[ti * 128:(ti + 1) * 128, :], hin[r][:, :], R=[b_hin[r]], W=[dst_bufs[ti]])
                nxt = ti + 8
                if nxt < ntile:
                    load_tile(nxt)


def hgrn_pass(k, c, T, src, src_bufs, dst, dst_bufs, w, lng_d, lnb_d, tag):
    nc = k.nc
    LN2 = float(np.log(2.0))
    with ExitStack() as es:
        w_in = sb(nc, es, "hw_in_sb", [128, 8, 4096], BF16)
        w_o = sb(nc, es, "hw_o_sb", [128, 8, D], BF16)
        A_bc = sb(nc, es, "hA", [128, D], F32)
        B_bc = sb(nc, es, "hB", [128, D], F32)
        gn_bc = sb(nc, es, "hgn", [128, D], F32)
        gbc = sb(nc, es, "gbc" + tag, [128, D], F32)
        bbc = sb(nc, es, "bbc" + tag, [128, D], F32)
        U = [sb(nc, es, "hU%d" % i, [128, 128], F32) for i in range(4)]
        nln2 = sb(nc, es, "hnln2", [128, 1], F32)
        mtri = sb(nc, es, "hmtri", [128, 128], BF16)
        ones = sb(nc, es, "hones", [128, 1], F32)
        NR = 3
        hin = [sb(nc, es, "hhin%d" % i, [128, D], F32) for i in range(NR)]
        xT = sb(nc, es, "hxT", [128, 8, 128], BF16)
        th = [sb(nc, es, "hth%d" % i, [128, 512], F32) for i in range(2)]
        q2 = sb(nc, es, "hq2", [128, D], F32)
        fg = sb(nc, es, "hfg", [128, D], F32)
        kk = sb(nc, es, "hkk", [128, D], F32)
        logf = sb(nc, es, "hlogf", [128, D], F32)
        et = [sb(nc, es, "het%d" % i, [128, D], F32) for i in range(2)]
        qh = sb(nc, es, "hqh", [128, D], BF16)
        qi = sb(nc, es, "hqi", [128, D], BF16)
        kmix = sb(nc, es, "hkmix", [128, D], BF16)
        kta = sb(nc, es, "hkta", [128, D], BF16)
        gg = [sb(nc, es, "hgg%d" % i, [128, D], F32) for i in range(2)]
        vb = [sb(nc, es, "hvb%d" % i, [128, D], BF16) for i in range(2)]
        kh = [sb(nc, es, "hkh%d" % i, [128, D], BF16) for i in range(2)]
        qhT = [sb(nc, es, "hqhT%d" % i, [128, 8, 128], BF16) for i in range(2)]
        qiT = [sb(nc, es, "hqiT%d" % i, [128, 8, 128], BF16) for i in range(2)]
        kmixT = [sb(nc, es, "hkmixT%d" % i, [128, 8, 128], BF16) for i in range(2)]
        ktaT = [sb(nc, es, "hktaT%d" % i, [128, 8, 128], BF16) for i in range(2)]
        dec = [sb(nc, es, "hdec%d" % i, [128, 8], F32) for i in range(2)]
        attn = sb(nc, es, "hattn", [128, 8, 128], BF16)
        S = sb(nc, es, "hS", [128, 8, 128], F32)
        Sb = sb(nc, es, "hSb", [128, 8, 128], BF16)
        sq = sb(nc, es, "hsq", [128, 128], BF16)
        ssum = sb(nc, es, "hssum", [128, 8], F32)
        rstd = sb(nc, es, "hrstd", [128, 8], F32)
        on = sb(nc, es, "hon", [128, 8, 128], F32)
        ogb = sb(nc, es, "hogb", [128, D], BF16)
        ogT = sb(nc, es, "hogT", [128, 8, 128], BF16)
        Ab = [es.enter_context(nc.psum_tensor("hA%d" % i, [128, 512], F32)) for i in range(2)]
        b_A = k.bufs(2, "hAb")
        c.tp = [es.enter_context(nc.psum_tensor("htp", [128, 1024], BF16))]
        c.b_tp = k.bufs(1, "htp")
        misc = es.enter_context(nc.psum_tensor("hmisc", [128, 512], F32))
        b_misc = k.buf()
        Yb = [es.enter_context(nc.psum_tensor("hY%d" % i, [128, 512], F32)) for i in range(4)]
        b_Y = k.bufs(4, "hY")
        b_w, b_wo, b_cst, b_gb = k.bufs(4, "hw")
        b_hin = k.bufs(NR, "hhin")
        b_xT = k.buf()
        b_th = k.bufs(2)
        (b_q2, b_fg, b_kk, b_logf, b_qh, b_qi, b_kmix, b_kta, b_attn, b_S, b_Sb, b_sq, b_ssum, b_rstd, b_on,
         b_ogb, b_ogT) = k.bufs(17, "hh")
        b_et = k.bufs(2)
        b_gg, b_vb, b_kh, b_qhT, b_qiT, b_kmixT, b_ktaT, b_dec = [k.bufs(2, "hx%d" % j) for j in range(8)]

        ntile = T // 128

        def load_tile(ti):
            r = ti % NR
            k.dma("sp", hin[r][:, :], src[ti * 128:(ti + 1) * 128, :], R=[src_bufs[ti]], W=[b_hin[r]])

        for ti in range(min(NR, ntile)):
            load_tile(ti)
        b_wj = k.bufs(8, "hwj")
        w3 = w["hw_in"].rearrange("p (c f) -> p c f", c=8)
        for j in range(8):
            k.dma("pool", w_in[:, :, j * 512:(j + 1) * 512], w3[:, :, j * 512:(j + 1) * 512], W=[b_wj[j]])
        for kc in range(0, 8, 2):
            k.dma("pool", w_o[:, kc:kc + 2, :], w["hw_o"][:, kc * D:(kc + 2) * D].rearrange("p (a d) -> p a d", a=2),
                  W=[b_wo])
        for i in range(4):
            k.dma("sp", U[i][:, :], w["hU"][i, :, :], W=[b_cst])
        k.op("dve", lambda e: e.memset(nln2[:, :], -LN2), W=[b_cst])
        k.dma("pool", mtri[:, :], w["hmtri"][:, :], W=[b_cst])
        k.dma("sp", ones[:, :], w["hones"][:, :], W=[b_cst])
        load_bcast(k, "sp", gbc[:, :], lng_d, b_gb)
        load_bcast(k, "sp", bbc[:, :], lnb_d, b_gb)
        load_bcast(k, "sp", gn_bc[:, :], w["hgn"][0:1, :], b_gb)
        load_bcast(k, "sp", A_bc[:, :], w["hlbl"][0:1, :], b_cst)
        load_bcast(k, "sp", B_bc[:, :], w["hlbl"][1:2, :], b_cst)
        k.op("dve", lambda e: e.tensor_tensor(out=B_bc[:, :], in0=B_bc[:, :], in1=A_bc[:, :], op=ALU.subtract),
             R=[b_cst], W=[b_cst])
        k.op("act", lambda e: e.activation(out=B_bc[:, :], in_=B_bc[:, :], func=AF.Tanh, scale=0.5), R=[b_cst], W=[b_cst])
        k.op("dve", lambda e: e.tensor_scalar(out=A_bc[:, :], in0=B_bc[:, :], scalar1=0.25, scalar2=0.75,
                                              op0=ALU.mult, op1=ALU.add), R=[b_cst], W=[b_cst])
        k.op("dve", lambda e: e.tensor_scalar(out=B_bc[:, :], in0=B_bc[:, :], scalar1=-0.25, scalar2=0.25,
                                              op0=ALU.mult, op1=ALU.add), R=[b_cst], W=[b_cst])
        k.op("dve", lambda e: e.memset(S[:, :, :], 0.0), W=[b_S])
        k.op("pool", lambda e: e.memset(Sb[:, :, :], 0.0), W=[b_Sb])
        k.op("pool", lambda e: e.memset(kta[:, :], 0.0), W=[b_kta])
        thi_box = [0]
        hfill_src = sb(nc, es, "hfill", [128, 256], BF16)
        k.op("pool", lambda e: e.memset(hfill_src[:, :], 1.0), W=[c.b_fill])

        def hfiller(n=1):
            for _ in range(n):
                k.op("pe", lambda e: e.matmul(misc[:, 256:512], c.ident[:, :], hfill_src[:, :], start=True, stop=True),
                     R=[c.b_const, c.b_fill], W=[c.b_fillps])

        def gen_X(ti):
            r = ti % NR
            p_ = ti % 2
            to_feature_major(k, c, hin[r][:, :], b_hin[r], xT, b_xT, 0, cast_eng="act")
            yield 2.0
            for j in range(8):
                p, bp = Ab[j % 2], b_A[j % 2]
                for kc in range(8):
                    k.op("pe", lambda e, kc=kc: e.matmul(p[:, :], xT[:, kc, :], w_in[:, kc, j * 512:(j + 1) * 512],
                                                         start=(kc == 0), stop=(kc == 7)), R=[b_xT, b_wj[j]], W=[bp])
                hfiller(FILL_H)
                cs = slice((j % 2) * 512, (j % 2) * 512 + 512)
                if j in (0, 1, 2, 3, 6, 7):
                    t_, bt_ = th[thi_box[0] % 2], b_th[thi_box[0] % 2]
                    thi_box[0] += 1
                    k.op("act", lambda e: e.activation(out=t_[:, :], in_=p[:, :], func=AF.Tanh, scale=0.5),
                         R=[bp], W=[bt_])
                if j in (0, 1):
                    k.op("dve", lambda e: e.scalar_tensor_tensor(out=q2[:, cs], in0=t_[:, :], scalar=1.0, in1=p[:, :],
                                                                 op0=ALU.add, op1=ALU.mult), R=[bt_, bp], W=[b_q2])
                elif j in (2, 3):
                    k.op("dve", lambda e: e.tensor_tensor(out=fg[:, cs], in0=t_[:, :], in1=B_bc[:, cs], op=ALU.mult),
                         R=[bt_, b_cst], W=[b_fg])
                    k.op("dve", lambda e: e.tensor_tensor(out=fg[:, cs], in0=fg[:, cs], in1=A_bc[:, cs], op=ALU.add),
                         R=[b_fg, b_cst], W=[b_fg])
                elif j in (4, 5):
                    k.op("act", lambda e: e.copy(out=vb[p_][:, cs], in_=p[:, :]), R=[bp], W=[b_vb[p_]])
                else:
                    k.op("dve", lambda e: e.scalar_tensor_tensor(out=gg[p_][:, cs], in0=t_[:, :], scalar=1.0, in1=p[:, :],
                                                                 op0=ALU.add, op1=ALU.mult), R=[bt_, bp], W=[b_gg[p_]])
                yield 2.0
            k.op("act", lambda e: e.activation(out=logf[:, :], in_=fg[:, :], func=AF.Ln), R=[b_fg], W=[b_logf])
            k.op("dve", lambda e: e.tensor_scalar(out=kk[:, :], in0=fg[:, :], scalar1=-1.0, scalar2=1.0,
                                                   op0=ALU.mult, op1=ALU.add), R=[b_fg], W=[b_kk])
            yield 2.0

            def cums(m):
                for hf in range(2):
                    p, bp = Ab[hf], b_A[hf]
                    k.op("pe", lambda e, m=m, hf=hf, p=p: e.matmul(p[:, :], U[m][:, :], logf[:, hf * 512:(hf + 1) * 512],
                                                                    start=True, stop=True), R=[b_logf, b_cst], W=[bp])
            for h in range(8):
                k.op("pe", lambda e, h=h: e.matmul(misc[:, h:h + 1], logf[:, h * 128:(h + 1) * 128], ones[:, :],
                                                   start=True, stop=True), R=[b_logf, b_cst], W=[b_misc])
            k.op("act", lambda e: e.activation(out=dec[p_][:, :], in_=misc[:, 0:8], func=AF.Exp), R=[b_misc], W=[b_dec[p_]])
            e0, be0 = et[0], b_et[0]
            e1, be1 = et[1], b_et[1]
            cums(0)
            for hf in range(2):
                cs = slice(hf * 512, hf * 512 + 512)
                k.op("act", lambda e, cs=cs, hf=hf: e.activation(out=e0[:, cs], in_=Ab[hf][:, :], func=AF.Exp, bias=nln2[:, 0:1]),
                     R=[b_A[hf], b_cst], W=[be0])
                k.op("dve", lambda e, cs=cs: e.tensor_tensor(out=qh[:, cs], in0=q2[:, cs], in1=e0[:, cs], op=ALU.mult),
                     R=[b_q2, be0], W=[b_qh])
                k.op("act", lambda e, cs=cs, hf=hf: e.activation(out=e1[0:64, cs], in_=Ab[hf][0:64, :], func=AF.Exp, scale=-1.0),
                     R=[b_A[hf]], W=[be1])
                k.op("dve", lambda e, cs=cs: e.tensor_tensor(out=kta[0:64, cs], in0=kk[0:64, cs], in1=e1[0:64, cs], op=ALU.mult),
                     R=[b_kk, be1], W=[b_kta])
            yield 4.0
            cums(3)
            for hf in range(2):
                cs = slice(hf * 512, hf * 512 + 512)
                k.op("act", lambda e, cs=cs, hf=hf: e.activation(out=e0[:, cs], in_=Ab[hf][:, :], func=AF.Exp, bias=nln2[:, 0:1]),
                     R=[b_A[hf], b_cst], W=[be0])
                k.op("dve", lambda e, cs=cs: e.tensor_tensor(out=qi[:, cs], in0=q2[:, cs], in1=e0[:, cs], op=ALU.mult),
                     R=[b_q2, be0], W=[b_qi])
            yield 3.0
            cums(1)
            for hf in range(2):
                cs = slice(hf * 512, hf * 512 + 512)
                k.op("act", lambda e, cs=cs, hf=hf: e.activation(out=e1[:, cs], in_=Ab[hf][:, :], func=AF.Exp),
                     R=[b_A[hf]], W=[be1])
                k.op("dve", lambda e, cs=cs: e.tensor_tensor(out=kh[p_][:, cs], in0=kk[:, cs], in1=e1[:, cs], op=ALU.mult),
                     R=[b_kk, be1], W=[b_kh[p_]])
            yield 3.0
            cums(2)
            for hf in range(2):
                cs = slice(hf * 512, hf * 512 + 512)
                k.op("act", lambda e, cs=cs, hf=hf: e.activation(out=e0[:, cs], in_=Ab[hf][:, :], func=AF.Exp),
                     R=[b_A[hf]], W=[be0])
                k.op("dve", lambda e, cs=cs: e.tensor_tensor(out=kmix[:, cs], in0=kk[:, cs], in1=e0[:, cs], op=ALU.mult),
                     R=[b_kk, be0], W=[b_kmix])
            yield 3.0
            tgts = [(c.tp[0][:, :], c.b_tp[0]), (Ab[0][:, :].bitcast(BF16), b_A[0]), (Ab[1][:, :].bitcast(BF16), b_A[1]),
                    (c.tp[0][:, :], c.b_tp[0])]
            for ri, (srcT, bsrc, dT, bd) in enumerate(((qh, b_qh, qhT[p_], b_qhT[p_]), (qi, b_qi, qiT[p_], b_qiT[p_]),
                                                       (kta, b_kta, ktaT[p_], b_ktaT[p_]), (kmix, b_kmix, kmixT[p_], b_kmixT[p_]))):
                tp, tp_b = tgts[ri]
                for h in range(8):
                    k.op("pe", lambda e, h=h, srcT=srcT: e.transpose(out=tp[:, h * 128:(h + 1) * 128],
                                                                      in_=srcT[:, h * 128:(h + 1) * 128],
                                                                      identity=c.ident[:, :]),
                         R=[bsrc, c.b_const], W=[tp_b])
                k.op("act", lambda e, dT=dT: e.copy(out=dT[:, :, :], in_=tp[:, :].rearrange("p (h t) -> p h t", h=8)),
                     R=[tp_b], W=[bd])
                yield 1.5

        def gen_Y(ti):
            r = ti % NR
            p_ = ti % 2
            for h in range(8):
                p, bp = Yb[h // 4], b_Y[h // 4]
                c0 = (h % 4) * 128
                k.op("pe", lambda e, h=h, p=p, c0=c0: e.matmul(p[:, c0:c0 + 64], ktaT[p_][:, h, :], qhT[p_][:, h, 0:64],
                                                                start=True, stop=True),
                     R=[b_ktaT[p_], b_qhT[p_]], W=[bp])
                k.op("pe", lambda e, h=h, p=p, c0=c0: e.matmul(p[:, c0 + 64:c0 + 128], kmixT[p_][:, h, :], qhT[p_][:, h, 64:128],
                                                                start=True, stop=True),
                     R=[b_kmixT[p_], b_qhT[p_]], W=[bp])
            for q4 in range(2):
                k.op("dve", lambda e, q4=q4: e.tensor_tensor(
                    out=attn[:, q4 * 4:(q4 + 1) * 4, :], in0=Yb[q4][:, :].rearrange("p (h t) -> p h t", h=4),
                    in1=mtri[:, :].unsqueeze(1).to_broadcast([128, 4, 128]), op=ALU.mult),
                    R=[b_Y[q4], b_cst], W=[b_attn])
            yield 3.0
            for h in range(8):
                p, bp = Yb[2 + h // 4], b_Y[2 + h // 4]
                c0 = (h % 4) * 128
                k.op("pe", lambda e, h=h, p=p, c0=c0: e.matmul(p[:, c0:c0 + 128], attn[:, h, :], vb[p_][:, h * 128:(h + 1) * 128],
                                                                start=True, stop=False), R=[b_attn, b_vb[p_]], W=[bp])
                k.op("pe", lambda e, h=h, p=p, c0=c0: e.matmul(p[:, c0:c0 + 128], qiT[p_][:, h, :], Sb[:, h, :],
                                                                start=False, stop=True), R=[b_qiT[p_], b_Sb], W=[bp])
            yield 2.0
            for h in range(8):
                p, bp = Yb[h // 4], b_Y[h // 4]
                c0 = (h % 4) * 128
                k.op("pe", lambda e, h=h, p=p, c0=c0: e.matmul(p[:, c0:c0 + 128], kh[p_][:, h * 128:(h + 1) * 128],
                                                                vb[p_][:, h * 128:(h + 1) * 128], start=True, stop=True),
                     R=[b_kh[p_], b_vb[p_]], W=[bp])
            for h in range(8):
                p, bp = Yb[h // 4], b_Y[h // 4]
                c0 = (h % 4) * 128
                k.op("dve", lambda e, h=h, p=p, c0=c0: e.scalar_tensor_tensor(
                    out=S[:, h, :], in0=S[:, h, :], scalar=dec[p_][:, h:h + 1], in1=p[:, c0:c0 + 128],
                    op0=ALU.mult, op1=ALU.add), R=[b_S, b_dec[p_], bp], W=[b_S])
            k.op("pool", lambda e: e.tensor_copy(out=Sb[:, :, :], in_=S[:, :, :]), R=[b_S], W=[b_Sb])
            yield 4.0
            for h in range(8):
                k.op("act", lambda e, h=h: e.activation(out=sq[:, :], in_=Yb[2 + h // 4][:, (h % 4) * 128:(h % 4) * 128 + 128],
                                                        func=AF.Square, accum_out=ssum[:, h:h + 1]),
                     R=[b_Y[2 + h // 4]], W=[b_sq, b_ssum])
            k.op("dve", lambda e: e.tensor_scalar(out=ssum[:, :], in0=ssum[:, :], scalar1=4.0 / 128, scalar2=4.0 * RMS_EPS,
                                                  op0=ALU.mult, op1=ALU.add), R=[b_ssum], W=[b_ssum])
            k.op("pool", lambda e: e.tensor_tensor(out=rstd[:, :], in0=ssum[:, :], in1=c.mhalf[:, 0:8], op=ALU.pow),
                 R=[b_ssum, c.b_mh], W=[b_rstd])
            for hf in range(2):
                k.op("dve", lambda e, hf=hf: e.tensor_tensor(
                    out=on[:, hf * 4:(hf + 1) * 4, :], in0=Yb[2 + hf][:, :].rearrange("p (h v) -> p h v", h=4),
                    in1=rstd[:, hf * 4:(hf + 1) * 4].unsqueeze(2).to_broadcast([128, 4, 128]), op=ALU.mult),
                    R=[b_Y[2 + hf], b_rstd], W=[b_on])
            onf = on[:, :, :].rearrange("p h v -> p (h v)")
            k.op("dve", lambda e: e.tensor_tensor(out=onf, in0=onf, in1=gn_bc[:, :], op=ALU.mult), R=[b_on, b_gb], W=[b_on])
            k.op("dve", lambda e: e.tensor_tensor(out=ogb[:, :], in0=onf, in1=gg[p_][:, :], op=ALU.mult), R=[b_on, b_gg[p_]], W=[b_ogb])
            yield 4.0
            ytp = Yb[2][:, :].bitcast(BF16)
            for ch in range(8):
                k.op("pe", lambda e, ch=ch: e.transpose(out=ytp[:, ch * 128:(ch + 1) * 128], in_=ogb[:, ch * 128:(ch + 1) * 128],
                                                        identity=c.ident[:, :]), R=[b_ogb, c.b_const], W=[b_Y[2]])
            k.op("act", lambda e: e.copy(out=ogT[:, :, :], in_=ytp[:, :].rearrange("p (c t) -> p c t", c=8)), R=[b_Y[2]], W=[b_ogT])
            yield 2.0
            for half in range(2):
                p, bp = Yb[half], b_Y[half]
                for kc in range(8):
                    k.op("pe", lambda e, kc=kc, p=p: e.matmul(p[:, :], ogT[:, kc, :], w_o[:, kc, half * 512:(half + 1) * 512],
                                                               start=(kc == 0), stop=(kc == 7)), R=[b_ogT, b_wo], W=[bp])
                k.op("dve", lambda e, p=p: e.scalar_tensor_tensor(
                    out=hin[r][:, half * 512:(half + 1) * 512], in0=hin[r][:, half * 512:(half + 1) * 512],
                    scalar=ALPHA, in1=p[:, :], op0=ALU.mult, op1=ALU.add), R=[b_hin[r], bp], W=[b_hin[r]])
                yield 2.0
            layer_norm_inplace(k, c, hin[r][:, :], b_hin[r], gbc[:, :], bbc[:, :], b_gb, tag)
            k.dma("sp", dst[ti * 128:(ti + 1) * 128, :], hin[r][:, :], R=[b_hin[r]], W=[dst_bufs[ti]])
            if ti + NR < ntile:
                load_tile(ti + NR)
            yield 5.0

        run_streams([gen_X(0)])
        for ti in range(ntile):
            gens = [gen_Y(ti)]
            if ti + 1 < ntile:
                gens.append(gen_X(ti + 1))
            run_streams(gens)


NIT = 13
FILL_N = 512
FILL_IDX = True
FILL_ATT = 1
FILL_H = 2
MASK_NEG = -30000.0


def rope_inplace(k, xt, b_x, H, Dh, half, cos, sin, b_tab, tc_, ts_, b_tc, b_ts):
    xv = xt.rearrange("p (h d) -> p h d", h=H)[:, :, 0:2 * half].rearrange("p h (two f) -> p h two f", two=2)
    tcv = tc_[:, 0:H * 2 * half].rearrange("p (h two f) -> p h two f", h=H, two=2)
    tsv = ts_[:, 0:H * 2 * half].rearrange("p (h two f) -> p h two f", h=H, two=2)
    cb = cos.unsqueeze(1).unsqueeze(1).to_broadcast([128, H, 2, half])
    sbb = sin.unsqueeze(1).unsqueeze(1).to_broadcast([128, H, 2, half])
    k.op("dve", lambda e: e.tensor_tensor(out=tcv, in0=xv, in1=cb, op=ALU.mult), R=[b_x, b_tab], W=[b_tc])
    k.op("pool", lambda e: e.tensor_tensor(out=tsv, in0=xv, in1=sbb, op=ALU.mult), R=[b_x, b_tab], W=[b_ts])
    k.op("dve", lambda e: e.tensor_tensor(out=xv[:, :, 0, :], in0=tcv[:, :, 0, :], in1=tsv[:, :, 1, :], op=ALU.subtract),
         R=[b_tc, b_ts], W=[b_x])
    k.op("pool", lambda e: e.tensor_tensor(out=xv[:, :, 1, :], in0=tcv[:, :, 1, :], in1=tsv[:, :, 0, :], op=ALU.add),
         R=[b_tc, b_ts], W=[b_x])


def dsa_pass(k, c, T, src, src_bufs, dst, dst_bufs, w, lng_d, lnb_d, tag):
    nc = k.nc
    ntile = T // 128
    topk = min(256, T // 4)
    PI = float(np.pi)
    with ExitStack() as es:
        w_in = sb(nc, es, "aw_in", [128, 8, 648], BF16)
        w_uq = sb(nc, es, "aw_uq", [128, 2, D], BF16)
        w_iq = sb(nc, es, "aw_iq", [128, 2, D], BF16)
        w_o = sb(nc, es, "aw_o", [128, 8, D], BF16)
        gbc = sb(nc, es, "gbc" + tag, [128, D], F32)
        bbc = sb(nc, es, "bbc" + tag, [128, D], F32)
        gcq = sb(nc, es, "agcq", [128, 256], F32)
        gik = sb(nc, es, "agik", [128, 128], F32)
        bik = sb(nc, es, "abik", [128, 128], F32)
        dmask = sb(nc, es, "admask", [128, 128], F32)
        identf = sb(nc, es, "aidentf", [128, 128], F32)
        ones_f = sb(nc, es, "aones", [128, 4], F32)
        pow2 = sb(nc, es, "apow2", [128, NIT + 1], F32)
        invq = sb(nc, es, "ainvq", [128, 8], F32)
        invi = sb(nc, es, "ainvi", [128, 16], F32)
        posi = sb(nc, es, "aposi", [128, ntile], I32)
        posf = sb(nc, es, "aposf", [128, ntile], F32)
        cosq = sb(nc, es, "acosq", [128, ntile, 8], F32)
        sinq = sb(nc, es, "asinq", [128, ntile, 8], F32)
        cosi = sb(nc, es, "acosi", [128, ntile, 16], F32)
        sini = sb(nc, es, "asini", [128, ntile, 16], F32)
        b_rr = k.buf("arr")
        kT = sb(nc, es, "akT", [128, T], BF16)
        kidxT = sb(nc, es, "akidxT", [128, T], BF16)
        vaug0 = sb(nc, es, "avaug0", [128, ntile, 65], BF16)
        vaug1 = sb(nc, es, "avaug1", [128, ntile, 128], BF16)
        score2 = [sb(nc, es, "ascore%d" % j, [128, T], F32) for j in range(2)]
        amax = [sb(nc, es, "aamax%d" % j, [128, 1], F32) for j in range(2)]
        maskb = sb(nc, es, "amaskb", [128, T], BF16)
        junk = maskb
        maskT = [sb(nc, es, "amaskT%d" % j, [128, ntile, 128], BF16) for j in range(2)]
        NR = 5
        hin = [sb(nc, es, "ahin%d" % j, [128, D], F32) for j in range(NR)]
        xT = [sb(nc, es, "axT%d" % j, [128, 8, 128], BF16) for j in range(2)]
        cqn = sb(nc, es, "acqn", [128, 256], BF16)
        cqT = sb(nc, es, "acqT", [128, 2, 128], BF16)
        qf = sb(nc, es, "aqf", [128, D], F32)
        qif = sb(nc, es, "aqif", [128, D], F32)
        qb = sb(nc, es, "aqb", [128, D], BF16)
        qib = sb(nc, es, "aqib", [128, D], BF16)
        qT = [sb(nc, es, "aqT%d" % j, [128, 8, 128], BF16) for j in range(3)]
        qidxT = sb(nc, es, "aqidxT", [128, 8, 128], BF16)
        kf = sb(nc, es, "akf", [128, 128], F32)
        kif = sb(nc, es, "akif", [128, 128], F32)
        kb_ = sb(nc, es, "akb", [128, 256], BF16)
        tcs = sb(nc, es, "atcs", [128, 256], F32)
        tss = sb(nc, es, "atss", [128, 256], F32)
        sm = sb(nc, es, "asm", [128, 16], F32)
        lst = sb(nc, es, "alst", [128, 6], F32)
        lmv = sb(nc, es, "almv", [128, 2], F32)
        lrs = sb(nc, es, "alrs", [128, 2], F32)
        wabs = sb(nc, es, "awabs", [128, 8], F32)
        sgn = sb(nc, es, "asgn", [128, 8], F32)
        dsg = sb(nc, es, "adsg", [128, 8, 128], BF16)
        rsb = [sb(nc, es, "arsb%d" % j, [128, 512], BF16) for j in range(2)]
        esb = [sb(nc, es, "aesb%d" % j, [128, 4, 128], BF16) for j in range(4)]
        oT = sb(nc, es, "aoT", [128, 4, 512], F32)
        rden = sb(nc, es, "arden", [128, 16], F32)
        onrm = sb(nc, es, "aonrm", [128, 8, 128], BF16)
        steps = sb(nc, es, "asteps", [128, NIT + 1], F32)
        bis = sb(nc, es, "abis", [128, 8], F32)
        G = [es.enter_context(nc.psum_tensor("aG%d" % j, [128, 512], F32)) for j in range(2)]
        b_G = k.bufs(2, "aG")
        sc = es.enter_context(nc.psum_tensor("asc", [128, 512], F32))
        b_sc = k.buf()
        st2_t = es.enter_context(nc.psum_tensor("ast2", [128, 512], F32))
        c.tp = [G[0][:, :].bitcast(BF16)]
        c.b_tp = [b_G[0]]
        st = [es.enter_context(nc.psum_tensor("ast%d" % j, [128, 512], F32)) for j in range(2)]
        b_st = k.bufs(2, "ast")
        st3 = [st[0], st[1], st2_t]
        b_st3 = [b_st[0], b_st[1], k.buf("ast2")]
        ot = [es.enter_context(nc.psum_tensor("aot%d" % j, [128, 512], F32)) for j in range(2)]
        b_ot = k.bufs(2, "aot")
        G_q = st[0]
        b_Gq = b_st[0]
        fill_ps = ot[1]
        fill_src = sb(nc, es, "afill", [128, 512], BF16)
        k.op("pool", lambda e: e.memset(fill_src[:, :], 1.0), W=[c.b_fill])

        def filler(n=512):
            k.op("pe", lambda e: e.matmul(fill_ps[:, 0:n], c.ident[:, :], fill_src[:, 0:n], start=True, stop=True),
                 R=[c.b_const, c.b_fill], W=[c.b_fillps])
        (b_w, b_cst, b_gb, b_tab, b_kT, b_kidxT, b_vaug, b_score, b_junk, b_maskb, b_cqn, b_cqT,
         b_qf, b_qif, b_qb, b_qib, b_qidxT, b_kf, b_kif, b_kb, b_tc, b_ts, b_sm, b_wabs, b_sgn, b_dsg,
         b_oT, b_rden, b_onrm, b_steps, b_bis, b_mid, b_cnt, b_t, b_lst, b_lmv, b_lrs, b_lnm) = k.bufs(38, "aa")
        b_maskT = k.bufs(2, "amT")
        b_qT = k.bufs(3, "aqT")
        b_score2 = k.bufs(2, "asc2")
        b_amax = k.bufs(2, "aamx")
        b_hin = k.bufs(NR, "ahin")
        b_xT = k.bufs(2)
        b_rsb = k.bufs(2)
        b_esb = k.bufs(4)

        for kc in range(8):
            k.dma("pool", w_in[:, kc, :], w["aw_in"][:, kc * 648:(kc + 1) * 648], W=[b_w])
        k.dma("pool", w_uq[:, :, :], w["aw_uq"][:, :].rearrange("p (a d) -> p a d", a=2), W=[b_w])
        k.dma("pool", w_iq[:, :, :], w["aw_iq"][:, :].rearrange("p (a d) -> p a d", a=2), W=[b_w])
        for h0 in range(0, 8, 4):
            k.dma("pool", w_o[:, h0:h0 + 4, :], w["aw_o"][:, h0 * D:(h0 + 4) * D].rearrange("p (a d) -> p a d", a=4), W=[b_w])
        load_bcast(k, "sp", gbc[:, :], lng_d, b_gb)
        load_bcast(k, "sp", bbc[:, :], lnb_d, b_gb)
        load_bcast(k, "sp", gcq[:, :], w["agcq"][0:1, :], b_cst)
        load_bcast(k, "sp", gik[:, :], w["agik"][0:1, :], b_cst)
        load_bcast(k, "sp", bik[:, :], w["abik"][0:1, :], b_cst)
        load_bcast(k, "sp", invq[:, :], w["ainvq"][0:1, :], b_cst)
        load_bcast(k, "sp", invi[:, :], w["ainvi"][0:1, :], b_cst)
        load_bcast(k, "sp", pow2[:, :], w["apow2"][0:1, :], b_cst)
        k.dma("sp", dmask[:, :], w["admask"][:, :], W=[b_cst])
        k.dma("sp", identf[:, :], w["aidentf"][:, :], W=[b_cst])
        k.dma("sp", posi[:, :], w["apos"][:, :], W=[b_cst])
        k.op("dve", lambda e: e.memset(ones_f[:, :], 1.0), W=[b_cst])
        k.op("pool", lambda e: e.memset(vaug0[:, :, :], 1.0), W=[b_vaug])
        k.op("pool", lambda e: e.memset(vaug1[:, :, :], 0.0), W=[b_vaug])
        k.op("pool", lambda e: e.memset(vaug1[:, :, 0:1], 1.0), W=[b_vaug])
        rr_f = qf[:, 0:ntile * 16]
        rr_i = qif[:, 0:ntile * 16].bitcast(I32)
        k.op("dve", lambda e: e.tensor_copy(out=posf[:, :], in_=posi[:, :]), R=[b_cst], W=[b_tab])
        for (ct, st_, inv, half) in ((cosq, sinq, invq, 8), (cosi, sini, invi, 16)):
            pb = posf[:, :].unsqueeze(2).to_broadcast([128, ntile, half])
            ib = inv[:, :].unsqueeze(1).to_broadcast([128, ntile, half])
            k.op("dve", lambda e, st_=st_, pb=pb, ib=ib: e.tensor_tensor(out=st_[:, :, :], in0=pb, in1=ib, op=ALU.mult),
                 R=[b_tab, b_cst], W=[b_tab])
            k.op("dve", lambda e, ct=ct, st_=st_: e.tensor_scalar(out=ct[:, :, :], in0=st_[:, :, :], scalar1=PI / 2, scalar2=None,
                                                          op0=ALU.add), R=[b_tab], W=[b_tab])
            for tt in (st_, ct):
                n_ = ntile * half
                tf = tt[:, :, :].rearrange("p a b -> p (a b)")
                kf_ = rr_f[:, 0:n_]
                ki_ = rr_i[:, 0:n_]
                k.op("dve", lambda e, tf=tf, kf_=kf_: e.tensor_scalar(out=kf_, in0=tf, scalar1=1.0 / (2.0 * PI), scalar2=None, op0=ALU.mult),
                     R=[b_tab], W=[b_rr])
                k.op("dve", lambda e, kf_=kf_, ki_=ki_: e.tensor_copy(out=ki_, in_=kf_), R=[b_rr], W=[b_rr])
                k.op("dve", lambda e, kf_=kf_, ki_=ki_: e.tensor_copy(out=kf_, in_=ki_), R=[b_rr], W=[b_rr])
                k.op("dve", lambda e, tf=tf, kf_=kf_: e.scalar_tensor_tensor(out=tf, in0=kf_, scalar=-6.28125, in1=tf, op0=ALU.mult, op1=ALU.add),
                     R=[b_rr, b_tab], W=[b_tab])
                k.op("dve", lambda e, tf=tf, kf_=kf_: e.scalar_tensor_tensor(out=tf, in0=kf_, scalar=-(2.0 * PI - 6.28125), in1=tf,
                                                                           op0=ALU.mult, op1=ALU.add), R=[b_rr, b_tab], W=[b_tab])
                k.op("dve", lambda e, tf=tf, kf_=kf_: e.tensor_scalar(out=kf_, in0=tf, scalar1=PI, scalar2=-2.0 * PI, op0=ALU.is_gt, op1=ALU.mult),
                     R=[b_tab], W=[b_rr])
                k.op("dve", lambda e, tf=tf, kf_=kf_: e.tensor_tensor(out=tf, in0=tf, in1=kf_, op=ALU.add), R=[b_rr, b_tab], W=[b_tab])
                k.op("dve", lambda e, tf=tf, kf_=kf_: e.tensor_scalar(out=kf_, in0=tf, scalar1=-PI, scalar2=2.0 * PI, op0=ALU.is_lt, op1=ALU.mult),
                     R=[b_tab], W=[b_rr])
                k.op("dve", lambda e, tf=tf, kf_=kf_: e.tensor_tensor(out=tf, in0=tf, in1=kf_, op=ALU.add), R=[b_rr, b_tab], W=[b_tab])
                k.op("dve", lambda e, tf=tf: e.tensor_scalar(out=tf, in0=tf, scalar1=PI, scalar2=-PI, op0=ALU.min, op1=ALU.max),
                     R=[b_tab], W=[b_tab])
                k.op("act", lambda e, tf=tf: e.activation(out=tf, in_=tf, func=AF.Sin), R=[b_tab], W=[b_tab])

        k.barrier()

        def load_tile(ti):
            r = ti % NR
            k.dma("sp", hin[r][:, :], src[ti * 128:(ti + 1) * 128, :], R=[src_bufs[ti]], W=[b_hin[r]])

        for ti in range(min(NR, ntile)):
            load_tile(ti)
        WSC = (IDX_H ** -0.5) * (IDX_D ** -0.5)
        gi_box = [0]

        def gen_P1(i):
            r = i % NR
            W_ = (i + 1) * 128
            xt, b_xt = xT[i % 2], b_xT[i % 2]
            qT_, b_qT_ = qT[i % 3], b_qT[i % 3]
            score, b_score = score2[i % 2], b_score2[i % 2]
            to_feature_major(k, c, hin[r][:, :], b_hin[r], xt, b_xt, 0, cast_eng="act")
            tp, tp_b = c.tp[0], c.b_tp[0]
            for kc in range(8):
                k.op("pe", lambda e, kc=kc: e.matmul(G[0][:, :], xt[:, kc, :], w_in[:, kc, 0:512], start=(kc == 0), stop=(kc == 7)),
                     R=[b_xt, b_w], W=[b_G[0]])
            for kc in range(8):
                k.op("pe", lambda e, kc=kc: e.matmul(G[1][:, 0:136], xt[:, kc, :], w_in[:, kc, 512:648], start=(kc == 0), stop=(kc == 7)),
                     R=[b_xt, b_w], W=[b_G[1]])
            yield 3.0
            k.op("act", lambda e: e.activation(out=tcs[:, 0:256], in_=G[0][:, 0:256], func=AF.Square, accum_out=sm[:, 0:1]),
                 R=[b_G[0]], W=[b_tc, b_sm])
            k.op("dve", lambda e: e.tensor_scalar(out=sm[:, 1:2], in0=sm[:, 0:1], scalar1=1.0 / 256, scalar2=RMS_EPS,
                                                  op0=ALU.mult, op1=ALU.add), R=[b_sm], W=[b_sm])
            k.op("pool", lambda e: e.tensor_tensor(out=sm[:, 2:3], in0=sm[:, 1:2], in1=c.mhalf[:, 0:1], op=ALU.pow),
                 R=[b_sm, c.b_mh], W=[b_sm])
            k.op("dve", lambda e: e.scalar_tensor_tensor(out=cqn[:, :], in0=G[0][:, 0:256], scalar=sm[:, 2:3], in1=gcq[:, :],
                                                         op0=ALU.mult, op1=ALU.mult), R=[b_G[0], b_sm, b_cst], W=[b_cqn])
            for ch in range(2):
                k.op("pe", lambda e, ch=ch: e.transpose(out=tp[:, ch * 128:(ch + 1) * 128], in_=cqn[:, ch * 128:(ch + 1) * 128],
                                                        identity=c.ident[:, :]), R=[b_cqn, c.b_const], W=[tp_b])
            k.op("act", lambda e: e.copy(out=cqT[:, :, :], in_=tp[:, 0:256].rearrange("p (c t) -> p c t", c=2)), R=[tp_b], W=[b_cqT])
            yield 2.0
            k.op("act", lambda e: e.copy(out=kf[:, :], in_=G[0][:, 256:384]), R=[b_G[0]], W=[b_kf])
            k.op("act", lambda e: e.copy(out=vaug0[:, i, 0:64], in_=G[0][:, 384:448]), R=[b_G[0]], W=[b_vaug])
            k.op("act", lambda e: e.copy(out=vaug1[:, i, 64:128], in_=G[0][:, 448:512]), R=[b_G[0]], W=[b_vaug])
            rope_inplace(k, kf[:, :], b_kf, 2, 64, 8, cosq[:, i, :], sinq[:, i, :], b_tab, tcs, tss, b_tc, b_ts)
            k.op("act", lambda e: e.copy(out=kb_[:, 0:128], in_=kf[:, :]), R=[b_kf], W=[b_kb])
            yield 2.0
            k.op("dve", lambda e: e.bn_stats(out=lst[:, :], in_=G[1][:, 0:128]), R=[b_G[1]], W=[b_lst])
            k.op("dve", lambda e: e.bn_aggr(out=lmv[:, :], in_=lst[:, :]), R=[b_lst], W=[b_lmv])
            k.op("dve", lambda e: e.tensor_scalar(out=lmv[:, 1:2], in0=lmv[:, 1:2], scalar1=LN_EPS, scalar2=None, op0=ALU.add),
                 R=[b_lmv], W=[b_lmv])
            k.op("pool", lambda e: e.tensor_tensor(out=lrs[:, 0:1], in0=lmv[:, 1:2], in1=c.mhalf[:, 0:1], op=ALU.pow),
                 R=[b_lmv, c.b_mh], W=[b_lrs])
            k.op("dve", lambda e: e.scalar_tensor_tensor(out=lrs[:, 1:2], in0=lmv[:, 0:1], scalar=-1.0, in1=lrs[:, 0:1],
                                                         op0=ALU.mult, op1=ALU.mult), R=[b_lmv, b_lrs], W=[b_lnm])
            k.op("act", lambda e: e.activation(out=kif[:, :], in_=G[1][:, 0:128], func=AF.Identity, bias=lrs[:, 1:2],
                                               scale=lrs[:, 0:1]), R=[b_G[1], b_lrs, b_lnm], W=[b_kif])
            k.op("dve", lambda e: e.tensor_tensor(out=kif[:, :], in0=kif[:, :], in1=gik[:, :], op=ALU.mult), R=[b_kif, b_cst], W=[b_kif])
            k.op("dve", lambda e: e.tensor_tensor(out=kif[:, :], in0=kif[:, :], in1=bik[:, :], op=ALU.add), R=[b_kif, b_cst], W=[b_kif])
            rope_inplace(k, kif[:, :], b_kif, 1, 128, 16, cosi[:, i, :], sini[:, i, :], b_tab, tcs, tss, b_tc, b_ts)
            k.op("act", lambda e: e.copy(out=kb_[:, 128:256], in_=kif[:, :]), R=[b_kif], W=[b_kb])
            k.op("act", lambda e: e.activation(out=wabs[:, :], in_=G[1][:, 128:136], func=AF.Abs, scale=WSC),
                 R=[b_G[1]], W=[b_wabs])
            k.op("act", lambda e: e.activation(out=sgn[:, :], in_=G[1][:, 128:136], func=AF.Sign), R=[b_G[1]], W=[b_sgn])
            k.op("dve", lambda e: e.tensor_tensor(out=dsg[:, :, :], in0=c.ident[:, :].unsqueeze(1).to_broadcast([128, 8, 128]),
                                                  in1=sgn[:, :].unsqueeze(2).to_broadcast([128, 8, 128]), op=ALU.mult),
                 R=[c.b_const, b_sgn], W=[b_dsg])
            for ch in range(2):
                k.op("pe", lambda e, ch=ch: e.transpose(out=tp[:, (2 + ch) * 128:(3 + ch) * 128], in_=kb_[:, ch * 128:(ch + 1) * 128],
                                                        identity=c.ident[:, :]), R=[b_kb, c.b_const], W=[tp_b])
            k.op("dve", lambda e: e.tensor_copy(out=kT[:, i * 128:(i + 1) * 128], in_=tp[:, 256:384]), R=[tp_b], W=[b_kT])
            k.op("dve", lambda e: e.tensor_copy(out=kidxT[:, i * 128:(i + 1) * 128], in_=tp[:, 384:512]), R=[tp_b], W=[b_kidxT])
            yield 4.0
            for (wq, dstf, bdst) in ((w_uq, qf, b_qf), (w_iq, qif, b_qif)):
                for half in range(2):
                    for kc in range(2):
                        k.op("pe", lambda e, kc=kc, half=half, wq=wq: e.matmul(G[half][:, :], cqT[:, kc, :],
                                                                               wq[:, kc, half * 512:(half + 1) * 512],
                                                                               start=(kc == 0), stop=(kc == 1)),
                             R=[b_cqT, b_w], W=[b_G[half]])
                    k.op("act", lambda e, half=half, dstf=dstf: e.copy(out=dstf[:, half * 512:(half + 1) * 512], in_=G[half][:, :]),
                         R=[b_G[half]], W=[bdst])
            yield 3.0
            rope_inplace(k, qf[:, :], b_qf, 16, 64, 8, cosq[:, i, :], sinq[:, i, :], b_tab, tcs, tss, b_tc, b_ts)
            k.op("dve", lambda e: e.tensor_copy(out=qb[:, :], in_=qf[:, :]), R=[b_qf], W=[b_qb])
            rope_inplace(k, qif[:, :], b_qif, 8, 128, 16, cosi[:, i, :], sini[:, i, :], b_tab, tcs, tss, b_tc, b_ts)
            k.op("dve", lambda e: e.tensor_tensor(out=qib[:, :].rearrange("p (h d) -> p h d", h=8),
                                                  in0=qif[:, :].rearrange("p (h d) -> p h d", h=8),
                                                  in1=wabs[:, :].unsqueeze(2).to_broadcast([128, 8, 128]), op=ALU.mult),
                 R=[b_qif, b_wabs], W=[b_qib])
            yield 4.0
            for (srcb, bsrc, dT, bd) in ((qib, b_qib, qidxT, b_qidxT), (qb, b_qb, qT_, b_qT_)):
                for h in range(8):
                    k.op("pe", lambda e, h=h, srcb=srcb: e.transpose(out=tp[:, h * 128:(h + 1) * 128],
                                                                      in_=srcb[:, h * 128:(h + 1) * 128], identity=c.ident[:, :]),
                         R=[bsrc, c.b_const], W=[tp_b])
                k.op("act", lambda e, dT=dT: e.copy(out=dT[:, :, :], in_=tp[:, :].rearrange("p (h t) -> p h t", h=8)),
                     R=[tp_b], W=[bd])
                yield 1.5
            nkb = (W_ + 511) // 512
            stepsl = [(kb, h) for kb in range(nkb) for h in range(8)]
            base = gi_box[0]
            gi_box[0] += len(stepsl)

            def iq(sidx):
                kb, h = stepsl[sidx]
                wd = min(512, W_ - kb * 512)
                g_, bg_ = G[(base + sidx) % 2], b_G[(base + sidx) % 2]
                k.op("pe", lambda e: e.matmul(g_[:, 0:wd], qidxT[:, h, :], kidxT[:, kb * 512:kb * 512 + wd],
                                              start=True, stop=True), R=[b_qidxT, b_kidxT], W=[bg_])
            iq(0)
            for sidx, (kb, h) in enumerate(stepsl):
                wd = min(512, W_ - kb * 512)
                g_, bg_ = G[(base + sidx) % 2], b_G[(base + sidx) % 2]
                rs_, brs_ = rsb[(base + sidx) % 2], b_rsb[(base + sidx) % 2]
                if h % 4 != 3:
                    k.op("act", lambda e: e.activation(out=rs_[:, 0:wd], in_=g_[:, 0:wd], func=AF.Relu), R=[bg_], W=[brs_])
                else:
                    k.op("dve", lambda e: e.tensor_scalar(out=rs_[:, 0:wd], in0=g_[:, 0:wd], scalar1=0.0, scalar2=None, op0=ALU.max),
                         R=[bg_], W=[brs_])
                if sidx + 1 < len(stepsl):
                    iq(sidx + 1)
                if FILL_IDX:
                    filler(FILL_N)
                k.op("pe", lambda e: e.matmul(sc[:, 0:wd], dsg[:, h, :], rs_[:, 0:wd], start=(h == 0), stop=(h == 7)),
                     R=[b_dsg, brs_], W=[b_sc])
                if h % 2 == 1:
                    yield 1.0 * wd / 512
                if h == 7:
                    k.op("dve", lambda e: e.tensor_copy(out=score[:, kb * 512:kb * 512 + wd], in_=sc[:, 0:wd]), R=[b_sc], W=[b_score])
                    yield 0.6
            search = i * 128 >= topk
            if search:
                k.op("dve", lambda e: e.tensor_reduce(out=amax[i % 2][:, 0:1], in_=score[:, 0:W_], axis=AX.X, op=ALU.max,
                                                      apply_absolute_value=True), R=[b_score], W=[b_amax[i % 2]])
            k.op("dve", lambda e: e.tensor_tensor(out=score[:, i * 128:W_], in0=score[:, i * 128:W_], in1=dmask[:, :], op=ALU.add),
                 R=[b_score, b_cst], W=[b_score])
            yield 3.0 * W_ / 2048

        def gen_P2(i):
            W_ = (i + 1) * 128
            mT, b_mT = maskT[i % 2], b_maskT[i % 2]
            score, b_score = score2[i % 2], b_score2[i % 2]
            tp, tp_b = ot[1][:, :].bitcast(BF16), c.b_fillps
            search = i * 128 >= topk
            if search:
                k.op("dve", lambda e: e.tensor_scalar(out=bis[:, 1:2], in0=amax[i % 2][:, 0:1], scalar1=2.002, scalar2=2e-6,
                                                      op0=ALU.mult, op1=ALU.add), R=[b_amax[i % 2]], W=[b_bis])
                k.op("dve", lambda e: e.tensor_scalar(out=steps[:, :], in0=pow2[:, :], scalar1=bis[:, 1:2], scalar2=None,
                                                      op0=ALU.mult), R=[b_bis, b_cst], W=[b_steps])
                k.op("dve", lambda e: e.memset(bis[:, 2:3], 0.0), W=[b_mid])
                for n in range(NIT):
                    k.op("dve", lambda e: e.tensor_scalar(out=junk[:, 0:W_], in0=score[:, 0:W_], scalar1=bis[:, 2:3], scalar2=0.0,
                                                          op0=ALU.is_ge, op1=ALU.add, accum_out=bis[:, 3:4]),
                         R=[b_score, b_mid], W=[b_maskb, b_cnt])
                    k.op("dve", lambda e, n=n: e.tensor_scalar(out=bis[:, 4:5], in0=bis[:, 3:4], scalar1=float(topk) - 0.5,
                                                               scalar2=steps[:, n:n + 1], op0=ALU.is_ge, op1=ALU.mult),
                         R=[b_cnt, b_steps], W=[b_t])
                    k.op("dve", lambda e, n=n: e.scalar_tensor_tensor(out=bis[:, 2:3], in0=bis[:, 4:5], scalar=steps[:, n + 1:n + 2],
                                                                      in1=bis[:, 2:3], op0=ALU.subtract, op1=ALU.add),
                         R=[b_t, b_steps, b_mid], W=[b_mid])
                    yield 0.6 + 3.0 * W_ / 2048
                k.op("dve", lambda e: e.tensor_tensor(out=bis[:, 5:6], in0=bis[:, 2:3], in1=steps[:, NIT:NIT + 1], op=ALU.subtract),
                     R=[b_mid, b_steps], W=[b_bis])
                k.op("dve", lambda e: e.tensor_scalar(out=maskb[:, 0:W_], in0=score[:, 0:W_], scalar1=bis[:, 5:6], scalar2=MASK_NEG,
                                                      op0=ALU.is_lt, op1=ALU.mult), R=[b_score, b_bis], W=[b_maskb])
            else:
                k.op("dve", lambda e: e.tensor_scalar(out=maskb[:, 0:W_], in0=score[:, 0:W_], scalar1=-1e29, scalar2=MASK_NEG,
                                                      op0=ALU.is_lt, op1=ALU.mult), R=[b_score], W=[b_maskb])
            yield 2.0 * W_ / 2048
            for b0 in range(0, i + 1, 8):
                nb = min(8, i + 1 - b0)
                for j in range(nb):
                    k.op("pe", lambda e, j=j, b0=b0: e.transpose(out=tp[:, j * 128:(j + 1) * 128],
                                                                 in_=maskb[:, (b0 + j) * 128:(b0 + j + 1) * 128], identity=c.ident[:, :]),
                         R=[b_maskb, c.b_const], W=[tp_b])
                k.op("act", lambda e, b0=b0, nb=nb: e.copy(out=mT[:, b0:b0 + nb, :],
                                                           in_=tp[:, 0:nb * 128].rearrange("p (b t) -> p b t", b=nb)),
                     R=[tp_b], W=[b_mT])
                yield 1.5

        def gen_Q(i):
            r = i % NR
            qT_, b_qT_ = qT[i % 3], b_qT[i % 3]
            mT, b_mT = maskT[i % 2], b_maskT[i % 2]
            mm = 0
            for quad in range(4):
                g, half = quad // 2, quad % 2
                o_, bo_ = ot[0], b_ot[0]
                rows = slice(g * 64, (g + 1) * 64)
                qrhs = qT_[rows, half * 4:(half + 1) * 4, :]

                def qk(kb):
                    s_, bs_ = st3[kb % 3], b_st3[kb % 3]
                    k.op("pe", lambda e: e.matmul(s_[:, :], kT[rows, kb * 128:(kb + 1) * 128], qrhs, start=True, stop=False),
                         R=[b_kT, b_qT_], W=[bs_])
                    k.op("pe", lambda e: e.matmul(s_[:, :], c.ident[:, :], mT[:, kb, :].unsqueeze(1).to_broadcast([128, 4, 128]),
                                                  start=False, stop=True), R=[c.b_const, b_mT], W=[bs_])
                qk(0)
                if i >= 1:
                    qk(1)
                for kb in range(i + 1):
                    if kb + 2 <= i:
                        qk(kb + 2)
                    s_, bs_ = st3[kb % 3], b_st3[kb % 3]
                    e_, be_ = esb[kb % 4], b_esb[kb % 4]
                    k.op("act", lambda e, s_=s_, e_=e_: e.activation(out=e_[:, :, :], in_=s_[:, :].rearrange("p (h t) -> p h t", h=4),
                                                                      func=AF.Exp, scale=0.125), R=[bs_], W=[be_])
                    if FILL_ATT:
                        filler(FILL_N)
                        if FILL_ATT > 1:
                            filler(FILL_N)
                    if g == 0:
                        k.op("pe", lambda e, e_=e_, kb=kb: e.matmul(o_[0:65, :], vaug0[:, kb, :], e_[:, :, :].rearrange("p h t -> p (h t)"),
                                                                    start=(kb == 0), stop=(kb == i)), R=[b_vaug, be_], W=[bo_])
                    else:
                        k.op("pe", lambda e, e_=e_, kb=kb: e.matmul(o_[:, :], vaug1[:, kb, :], e_[:, :, :].rearrange("p h t -> p (h t)"),
                                                                    start=(kb == 0), stop=(kb == i)), R=[b_vaug, be_], W=[bo_])
                    yield 1.0
                if g == 0:
                    k.op("act", lambda e, o_=o_, quad=quad: e.copy(out=oT[0:65, quad, :], in_=o_[0:65, :]), R=[bo_], W=[b_oT])
                else:
                    k.op("act", lambda e, o_=o_, quad=quad: e.copy(out=oT[:, quad, :], in_=o_[:, :]), R=[bo_], W=[b_oT])
                yield 0.7
            for quad in range(4):
                dr = 64 if quad < 2 else 0
                for hh in range(4):
                    k.op("pe", lambda e, hh=hh, quad=quad, dr=dr: e.matmul(G_q[:, quad * 4 + hh:quad * 4 + hh + 1],
                                                                           oT[dr:dr + 1, quad, hh * 128:(hh + 1) * 128],
                                                                           ones_f[dr:dr + 1, 0:1], start=True, stop=True),
                         R=[b_oT, b_cst], W=[b_Gq])
            yield 1.0
            k.op("dve", lambda e: e.reciprocal(out=rden[:, :], in_=G_q[:, 0:16]), R=[b_Gq], W=[b_rden])
            for quad in range(4):
                g, half = quad // 2, quad % 2
                s_, bs_ = st[quad % 2], b_st[quad % 2]
                for hh in range(4):
                    idx = quad * 4 + hh
                    k.op("pe", lambda e, hh=hh, idx=idx, s_=s_: e.matmul(s_[:, hh * 128:(hh + 1) * 128],
                                                                          rden[:, idx:idx + 1].to_broadcast([128, 128]), identf[:, :],
                                                                          start=True, stop=True), R=[b_rden, b_cst], W=[bs_])
                rows = slice(g * 64, (g + 1) * 64)
                k.op("dve", lambda e, quad=quad, s_=s_, rows=rows, half=half: e.tensor_tensor(
                    out=onrm[rows, half * 4:(half + 1) * 4, :].rearrange("p h t -> p (h t)"), in0=oT[rows, quad, :],
                    in1=s_[rows, :], op=ALU.mult), R=[b_oT, bs_], W=[b_onrm])
                yield 1.5
            for half in range(2):
                o_, bo_ = (ot[0], b_ot[0]) if half == 0 else (st[1], b_st[1])
                for j in range(8):
                    k.op("pe", lambda e, j=j, half=half, o_=o_: e.matmul(o_[:, :], onrm[:, j, :], w_o[:, j, half * 512:(half + 1) * 512],
                                                                         start=(j == 0), stop=(j == 7)), R=[b_onrm, b_w], W=[bo_])
                k.op("dve", lambda e, half=half, o_=o_: e.scalar_tensor_tensor(
                    out=hin[r][:, half * 512:(half + 1) * 512], in0=hin[r][:, half * 512:(half + 1) * 512],
                    scalar=ALPHA, in1=o_[:, :], op0=ALU.mult, op1=ALU.add), R=[b_hin[r], bo_], W=[b_hin[r]])
                yield 2.0
            layer_norm_inplace(k, c, hin[r][:, :], b_hin[r], gbc[:, :], bbc[:, :], b_gb, tag)
            k.dma("sp", dst[i * 128:(i + 1) * 128, :], hin[r][:, :], R=[b_hin[r]], W=[dst_bufs[i]])
            if i + NR < ntile:
                load_tile(i + NR)
            yield 5.0

        run_streams([gen_P1(0)])
        g0 = [gen_P2(0)]
        if ntile > 1:
            g0.append(gen_P1(1))
        run_streams(g0)
        for i in range(ntile):
            gens = [gen_Q(i)]
            if i + 1 < ntile:
                gens.append(gen_P2(i + 1))
            if i + 2 < ntile:
                gens.append(gen_P1(i + 2))
            run_streams(gens)


def build_program(T=4096, passes=("ffn0",), dbg=False):
    nc = bass.Bass("TRN2", target_bir_lowering=False)
    ntile = T // 128
    dt = {}

    def din(name, shape, dtype=F32):
        dt[name] = nc.dram_tensor(name, list(shape), dtype, kind="ExternalInput").ap()
        return dt[name]

    x = din("x", [T, D])
    ident_d = din("ident", [128, 128])
    ffn_w = []
    for l in range(2):
        ffn_w.append((din("wg%d" % l, [128, 8 * DFF]), din("wu%d" % l, [128, 8 * DFF]),
                      din("wd%d" % l, [128, NF * D])))
    hw = {"hw_in": din("hw_in", [128, 8 * 4096]), "hw_o": din("hw_o", [128, 8 * D]),
          "hU": din("hU", [4, 128, 128]), "hmtri": din("hmtri", [128, 128]), "hones": din("hones", [128, 1]),
          "hgn": din("hgn", [1, D]), "hlbl": din("hlbl", [2, D])}
    aw = {"aw_in": din("aw_in", [128, 8 * 648]), "aw_uq": din("aw_uq", [128, 2 * D]), "aw_iq": din("aw_iq", [128, 2 * D]),
          "aw_o": din("aw_o", [128, 8 * D]), "aidentf": din("aidentf", [128, 128]), "agcq": din("agcq", [1, 256]), "agik": din("agik", [1, 128]),
          "abik": din("abik", [1, 128]), "ainvq": din("ainvq", [1, 8]), "ainvi": din("ainvi", [1, 16]),
          "apow2": din("apow2", [1, NIT + 1]), "admask": din("admask", [128, 128]), "apos": din("apos", [128, ntile], I32)}
    ln_g = din("ln_g", [4, D])
    ln_b = din("ln_b", [4, D])
    out = nc.dram_tensor("out", [T, D], F32, kind="ExternalOutput").ap()
    scr = [nc.dram_tensor("scr%d" % i, [T, D], F32, kind="Internal").ap() for i in range(3)]

    with ExitStack() as es:
        k = K(nc, es)
        c = Ctx()
        c.ident = sb(nc, es, "ident_sb", [128, 128], BF16)
        c.b_const = k.buf("const")
        k.dma("pool", c.ident[:, :], ident_d[:, :], W=[c.b_const])
        c.b_fill = k.buf("fill")
        c.b_fillps = k.buf("fillps")
        c.mhalf = sb(nc, es, "mhalf", [128, 16], F32)
        c.b_mh = k.buf("mhalf")
        k.op("dve", lambda e: e.memset(c.mhalf[:, :], -0.5), W=[c.b_mh])
        c.ln_stats = sb(nc, es, "ln_stats", [128, 2, 6], F32)
        c.ln_mv = sb(nc, es, "ln_mv", [128, 2], F32)
        c.ln_rs = sb(nc, es, "ln_rs", [128, 2], F32)
        c.b_lnst, c.b_lnmv, c.b_lnrs, c.b_lnnm = k.bufs(4, "ln")
        c.hb = [sb(nc, es, "hb%d" % i, [128, D], BF16) for i in range(1)]
        c.b_hb = k.bufs(1, "hb")
        c.tp_i = 0

        x_bufs = k.bufs(ntile, "x")
        scr_bufs = [k.bufs(ntile, "scr%d_" % i) for i in range(3)]
        out_bufs = k.bufs(ntile, "out")

        cur, cur_b = x, x_bufs
        plan = list(passes)
        for pi, p in enumerate(plan):
            last = pi == len(plan) - 1
            dstt, dst_b = (out, out_bufs) if last else (scr[pi % 3], scr_bufs[pi % 3])
            if p in ("ffn0", "ffn1"):
                l = int(p[-1])
                ffn_pass(k, c, T, cur, cur_b, dstt, dst_b, ffn_w[l][0], ffn_w[l][1], ffn_w[l][2],
                         ln_g[2 * l + 1:2 * l + 2, :], ln_b[2 * l + 1:2 * l + 2, :], es, p)
            elif p == "dsa":
                dsa_pass(k, c, T, cur, cur_b, dstt, dst_b, aw, ln_g[0:1, :], ln_b[0:1, :], p)
            elif p == "hgrn":
                hgrn_pass(k, c, T, cur, cur_b, dstt, dst_b, hw, ln_g[2:3, :], ln_b[2:3, :], p)
            cur, cur_b = dstt, dst_b
            if not last:
                k.barrier()
        k.finish(out_bufs)
    return nc


def wlayout(w, nchunk):
    n = w.shape[1]
    return np.ascontiguousarray(w.reshape(nchunk, 128, n).transpose(1, 0, 2).reshape(128, nchunk * n))


def make_in_maps(inputs, T=4096, ncores=8):
    f = np.float32
    shared = {"ident": np.eye(128, dtype=f)}
    for l in range(2):
        shared["wg%d" % l] = wlayout(np.asarray(inputs["ffn_w_gate"][l], f), 8)
        shared["wu%d" % l] = wlayout(np.asarray(inputs["ffn_w_up"][l], f), 8)
        shared["wd%d" % l] = wlayout(np.asarray(inputs["ffn_w_down"][l], f), NF)
    shared["aw_in"] = wlayout(np.asarray(inputs["att_w_in"][0], f), 8)
    perm = np.concatenate([np.r_[j * 64:(j + 1) * 64, (8 + j) * 64:(9 + j) * 64] for j in range(8)])
    shared["aw_uq"] = wlayout(np.asarray(inputs["att_w_uq"][0], f)[:, perm], 2)
    shared["aw_iq"] = wlayout(np.asarray(inputs["att_w_iq"][0], f), 2)
    shared["aw_o"] = np.ascontiguousarray(np.asarray(inputs["att_w_o"][0], f).reshape(2, 8, 64, D).transpose(0, 2, 1, 3).reshape(128, 8 * D))
    shared["aidentf"] = np.eye(128, dtype=f)
    shared["agcq"] = np.ascontiguousarray(np.asarray(inputs["att_g_cq"], f).reshape(1, 256))
    shared["agik"] = np.ascontiguousarray(np.asarray(inputs["att_g_ik"], f).reshape(1, 128))
    shared["abik"] = np.ascontiguousarray(np.asarray(inputs["att_b_ik"], f).reshape(1, 128))
    shared["ainvq"] = (np.float32(ROPE_THETA) ** (-np.arange(0, 16, 2, dtype=f) / np.float32(16))).astype(f)[None, :]
    shared["ainvi"] = (np.float32(ROPE_THETA) ** (-np.arange(0, 32, 2, dtype=f) / np.float32(32))).astype(f)[None, :]
    shared["apow2"] = (2.0 ** -(np.arange(NIT + 1, dtype=f) + 1)).astype(f)[None, :]
    qi_ = np.arange(128)[:, None]
    si_ = np.arange(128)[None, :]
    shared["admask"] = np.where(si_ <= qi_, 0.0, -1e30).astype(f)
    shared["hw_in"] = wlayout(np.asarray(inputs["hgrn_w_in"][0], f), 8)
    shared["hw_o"] = wlayout(np.asarray(inputs["hgrn_w_o"][0], f), 8)
    si = np.arange(128)[:, None]
    ti = np.arange(128)[None, :]
    same = (si // 64) == (ti // 64)
    U1 = (same & (si <= ti)).astype(f)
    U2 = (si > ti).astype(f)
    U3 = np.where(ti < 64, ((si < 64) & (si > ti)).astype(f), -((si >= 64) & (si <= ti)).astype(f)).astype(f)
    U4 = (si <= ti).astype(f)
    shared["hU"] = np.ascontiguousarray(np.stack([U1, U2, U3, U4]))
    shared["hmtri"] = (si <= ti).astype(f)
    shared["hones"] = np.ones((128, 1), f)
    shared["hgn"] = np.ascontiguousarray(np.tile(np.asarray(inputs["hgrn_g_norm"][0], f), 8)[None, :])
    shared["hlbl"] = np.ascontiguousarray(np.asarray(inputs["hgrn_lb_logits"], f))
    shared["ln_g"] = np.ascontiguousarray(np.asarray(inputs["ln_g"], f).reshape(4, D))
    shared["ln_b"] = np.ascontiguousarray(np.asarray(inputs["ln_b"], f).reshape(4, D))
    maps = []
    for b in range(ncores):
        m = dict(shared)
        m["x"] = np.ascontiguousarray(np.asarray(inputs["x"][b, :T], f))
        m["apos"] = np.ascontiguousarray(np.asarray(inputs["positions"][b, :T], np.int32).reshape(T // 128, 128).T)
        maps.append(m)
    return maps


def kernel(**inputs):
    T = 4096
    nc = build_program(T, passes=("dsa", "ffn0", "hgrn", "ffn1"))
    maps = make_in_maps(inputs, T, 8)
    res = run_bass_kernel_spmd(nc, maps, core_ids=list(range(8)))
    return np.stack([np.asarray(r["out"], np.float32) for r in res.results], axis=0)
```
